# Optimizing a Trainium2 kernel written in Bass

```python
import jax, jax.numpy as jnp
from jax import lax
import numpy as np

D_MODEL = 1024
BATCH = 16
SEQ = 2048
DEPTH = 2

N_A_LAYERS = DEPTH // 2
N_B_LAYERS = DEPTH - N_A_LAYERS

CHUNK = 64
EPS = 1e-6
NEG_INF = -1e30

HEADS_A = 16
HEAD_DIM_A = D_MODEL // HEADS_A
LEFT_CHUNKS = 8
BAND = (LEFT_CHUNKS + 1) * CHUNK
MAX_REL = 128

HEADS_B = D_MODEL // 128
NOPE_DIM = 128
ROPE_DIM = 64
V_DIM = 128
Q_LORA = 768
KV_LORA = 256
ROPE_THETA = 10000.0
Q_BLOCK = 128

D_FF = ((8 * D_MODEL // 3 + 127) // 128) * 128

kernel_name = "yoco_chunked_relpos_mla_macaron"


def rms_norm(x, g):
    xf = x.astype(jnp.float32)
    y = xf * lax.rsqrt(jnp.mean(xf * xf, axis=-1, keepdims=True) + EPS)
    return (y * g.astype(jnp.float32)).astype(x.dtype)


def swiglu(h, w_in, w_out):
    u = h @ w_in
    return (jax.nn.silu(u[..., :D_FF]) * u[..., D_FF:]) @ w_out


def rope_tables(seq_len):
    half = ROPE_DIM // 2
    freqs = ROPE_THETA ** (-jnp.arange(half, dtype=jnp.float32) / half)
    ang = jnp.arange(seq_len, dtype=jnp.float32)[:, None] * freqs[None, :]
    return jnp.cos(ang), jnp.sin(ang)


def apply_rope(x, cos, sin):
    half = ROPE_DIM // 2
    cos = cos.astype(x.dtype)
    sin = sin.astype(x.dtype)
    x1, x2 = x[..., :half], x[..., half:]
    return jnp.concatenate([x1 * cos - x2 * sin, x2 * cos + x1 * sin], axis=-1)


def chunked_relpos_attention(h, w_qkv, w_o, rel_table):
    B, S, _ = h.shape
    nc = S // CHUNK
    qkv = (h @ w_qkv).reshape(B, S, 3, HEADS_A, HEAD_DIM_A)
    q, k, v = qkv[:, :, 0], qkv[:, :, 1], qkv[:, :, 2]
    pad = LEFT_CHUNKS * CHUNK
    kp = jnp.pad(k, ((0, 0), (pad, 0), (0, 0), (0, 0)))
    vp = jnp.pad(v, ((0, 0), (pad, 0), (0, 0), (0, 0)))
    qi = jnp.arange(CHUNK)[:, None]
    kj = jnp.arange(BAND)[None, :]
    rel_idx = jnp.clip(pad + qi - kj, -MAX_REL, MAX_REL) + MAX_REL
    bias = rel_table[:, rel_idx].astype(jnp.float32)
    q_chunks = jnp.moveaxis(q.reshape(B, nc, CHUNK, HEADS_A, HEAD_DIM_A), 1, 0)
    scale = HEAD_DIM_A ** -0.5

    def one_chunk(args):
        qc, c = args
        start = c * CHUNK
        kb = lax.dynamic_slice_in_dim(kp, start, BAND, axis=1)
        vb = lax.dynamic_slice_in_dim(vp, start, BAND, axis=1)
        s = jnp.einsum('bqhd,bkhd->bhqk', qc, kb).astype(jnp.float32) * scale + bias
        valid = kj >= pad - start
        s = jnp.where(valid[None, None], s, NEG_INF)
        p = jax.nn.softmax(s, axis=-1).astype(vb.dtype)
        return jnp.einsum('bhqk,bkhd->bqhd', p, vb)

    out = lax.map(one_chunk, (q_chunks, jnp.arange(nc)))
    out = jnp.moveaxis(out, 0, 1).reshape(B, S, HEADS_A * HEAD_DIM_A)
    return out @ w_o


def mla_shared_kv(h_kv, w_down, latent_norm, w_up, cos, sin):
    B, S, _ = h_kv.shape
    ckr = h_kv @ w_down
    c_kv = rms_norm(ckr[..., :KV_LORA], latent_norm)
    k_rope = apply_rope(ckr[..., KV_LORA:], cos, sin)
    kv = (c_kv @ w_up).reshape(B, S, HEADS_B, NOPE_DIM + V_DIM)
    return kv[..., :NOPE_DIM], k_rope, kv[..., NOPE_DIM:]


def mla_attention(h, w_dq, q_norm, w_uq, w_o, k_nope, k_rope, v, cos, sin):
    B, S, _ = h.shape
    cq = rms_norm(h @ w_dq, q_norm)
    q = (cq @ w_uq).reshape(B, S, HEADS_B, NOPE_DIM + ROPE_DIM)
    q_nope = q[..., :NOPE_DIM]
    q_rope = apply_rope(q[..., NOPE_DIM:], cos[:, None], sin[:, None])
    nb = S // Q_BLOCK
    qn_blocks = jnp.moveaxis(q_nope.reshape(B, nb, Q_BLOCK, HEADS_B, NOPE_DIM), 1, 0)
    qr_blocks = jnp.moveaxis(q_rope.reshape(B, nb, Q_BLOCK, HEADS_B, ROPE_DIM), 1, 0)
    key_chunk = jnp.arange(S) // CHUNK
    scale = (NOPE_DIM + ROPE_DIM) ** -0.5

    def one_block(args):
        qn, qr, bidx = args
        s = (jnp.einsum('bqhd,bkhd->bhqk', qn, k_nope)
             + jnp.einsum('bqhr,bkr->bhqk', qr, k_rope)).astype(jnp.float32) * scale
        q_chunk = (bidx * Q_BLOCK + jnp.arange(Q_BLOCK)) // CHUNK
        mask = key_chunk[None, :] <= q_chunk[:, None]
        s = jnp.where(mask[None, None], s, NEG_INF)
        p = jax.nn.softmax(s, axis=-1).astype(v.dtype)
        return jnp.einsum('bhqk,bkhd->bqhd', p, v)

    out = lax.map(one_block, (qn_blocks, qr_blocks, jnp.arange(nb)))
    out = jnp.moveaxis(out, 0, 1).reshape(B, S, HEADS_B * V_DIM)
    return out @ w_o


def setup_inputs(seed: int = 0) -> dict:
    key = jax.random.key(seed)
    ks = jax.random.split(key, 24)

    def w(k, shape, fan_in):
        return jax.random.normal(k, shape, jnp.float32) * fan_in ** -0.5

    def gain(k, shape):
        return 1.0 + 0.05 * jax.random.normal(k, shape, jnp.float32)

    return {
        "x": jax.random.normal(ks[0], (BATCH, SEQ, D_MODEL), jnp.float32),
        "ffn1_norm": gain(ks[1], (DEPTH, D_MODEL)),
        "ffn1_w_in": w(ks[2], (DEPTH, D_MODEL, 2 * D_FF), D_MODEL),
        "ffn1_w_out": w(ks[3], (DEPTH, D_FF, D_MODEL), D_FF),
        "mix_norm": gain(ks[4], (DEPTH, D_MODEL)),
        "ffn2_norm": gain(ks[5], (DEPTH, D_MODEL)),
        "ffn2_w_in": w(ks[6], (DEPTH, D_MODEL, 2 * D_FF), D_MODEL),
        "ffn2_w_out": w(ks[7], (DEPTH, D_FF, D_MODEL), D_FF),
        "a_w_qkv": w(ks[8], (N_A_LAYERS, D_MODEL, 3 * HEADS_A * HEAD_DIM_A), D_MODEL),
        "a_rel_bias": 0.5 * jax.random.normal(ks[9], (N_A_LAYERS, HEADS_A, 2 * MAX_REL + 1), jnp.float32),
        "a_w_o": w(ks[10], (N_A_LAYERS, HEADS_A * HEAD_DIM_A, D_MODEL), HEADS_A * HEAD_DIM_A),
        "kv_norm": gain(ks[11], (D_MODEL,)),
        "kv_w_down": w(ks[12], (D_MODEL, KV_LORA + ROPE_DIM), D_MODEL),
        "kv_latent_norm": gain(ks[13], (KV_LORA,)),
        "kv_w_up": w(ks[14], (KV_LORA, HEADS_B * (NOPE_DIM + V_DIM)), KV_LORA),
        "b_w_dq": w(ks[15], (N_B_LAYERS, D_MODEL, Q_LORA), D_MODEL),
        "b_q_norm": gain(ks[16], (N_B_LAYERS, Q_LORA)),
        "b_w_uq": w(ks[17], (N_B_LAYERS, Q_LORA, HEADS_B * (NOPE_DIM + ROPE_DIM)), Q_LORA),
        "b_w_o": w(ks[18], (N_B_LAYERS, HEADS_B * V_DIM, D_MODEL), HEADS_B * V_DIM),
        "final_norm": gain(ks[19], (D_MODEL,)),
    }


def reference(x, ffn1_norm, ffn1_w_in, ffn1_w_out, mix_norm, ffn2_norm, ffn2_w_in,
              ffn2_w_out, a_w_qkv, a_rel_bias, a_w_o, kv_norm, kv_w_down,
              kv_latent_norm, kv_w_up, b_w_dq, b_q_norm, b_w_uq, b_w_o, final_norm):
    S = x.shape[1]
    cos, sin = rope_tables(S)
    h = x
    k_nope = k_rope = v_shared = None
    for layer in range(DEPTH):
        h = h + 0.5 * swiglu(rms_norm(h, ffn1_norm[layer]), ffn1_w_in[layer], ffn1_w_out[layer])
        hn = rms_norm(h, mix_norm[layer])
        if layer < N_A_LAYERS:
            h = h + chunked_relpos_attention(hn, a_w_qkv[layer], a_w_o[layer], a_rel_bias[layer])
        else:
            li = layer - N_A_LAYERS
            h = h + mla_attention(hn, b_w_dq[li], b_q_norm[li], b_w_uq[li], b_w_o[li],
                                  k_nope, k_rope, v_shared, cos, sin)
        h = h + 0.5 * swiglu(rms_norm(h, ffn2_norm[layer]), ffn2_w_in[layer], ffn2_w_out[layer])
        if layer == N_A_LAYERS - 1:
            k_nope, k_rope, v_shared = mla_shared_kv(rms_norm(h, kv_norm), kv_w_down,
                                                     kv_latent_norm, kv_w_up, cos, sin)
    return rms_norm(h, final_norm)
```

```python
import numpy as np
from contextlib import ExitStack
import concourse.bass as bass
import concourse.mybir as mybir
from concourse.bass_utils import run_bass_kernel_spmd

F32 = mybir.dt.float32
BF16 = mybir.dt.bfloat16
AF = mybir.ActivationFunctionType
ALU = mybir.AluOpType

NCORES = 8
D = 1024
S = 2048
NSEQ = 2
T = NSEQ * S
NSB = T // 512
DFF = 2816
NJ = 22
WT = 2816
EPS = 1e-6
ENGS = ['sp', 'pe', 'act', 'dve', 'pool']

G_FFN1 = [0, 8]
G_MIX = [16, 24]
G_FFN2 = [32, 40]
G_KV = 48
G_FINAL = 56
G_LAT = 64
G_QN = 66
NG = 80
CAST = 'dma'
SQ_ENG = 'act'
RSTD = 'lnexp'
RECIP = 'act'
A_RCP = 'dve'


class R:
    __slots__ = ('n', 'lw', 'rd')

    def __init__(self, n=''):
        self.n = n
        self.lw = None
        self.rd = []


class Op:
    __slots__ = ('eng', 'fn', 'deps', 'pos', 'chan', 'cidx', 'inc', 'cnt', 'rdeps')


class Sched:
    def __init__(self):
        self.ops = {e: [] for e in ENGS}
        self.chans = {}

    def add(self, eng, fn, rd=(), wr=(), chan=None):
        op = Op()
        op.eng = eng
        op.fn = fn
        op.chan = chan
        op.inc = False
        op.cnt = 0
        op.cidx = 0
        deps = set()
        for r in rd:
            if r.lw is not None:
                deps.add(r.lw)
        for r in wr:
            if r.lw is not None:
                deps.add(r.lw)
            deps.update(r.rd)
        op.deps = deps
        for r in rd:
            r.rd.append(op)
        for r in wr:
            r.lw = op
            r.rd = []
        op.pos = len(self.ops[eng])
        self.ops[eng].append(op)
        if chan is not None:
            c = self.chans.setdefault(chan, [0])
            c[0] += 1
            op.cidx = c[0]
        return op

    @staticmethod
    def reduce(deps, eng):
        best = {}
        for d in deps:
            if d.chan is not None:
                k = ('c', d.chan)
                if k not in best or d.cidx > best[k].cidx:
                    best[k] = d
            else:
                if d.eng == 'pe' and eng == 'pe':
                    continue
                k = d.eng
                if k not in best or d.pos > best[k].pos:
                    best[k] = d
        return list(best.values())

    def emit(self, nc, stack, final_chans):
        for e in ENGS:
            for op in self.ops[e]:
                op.rdeps = self.reduce(op.deps, e)
                op.deps = None
                for d in op.rdeps:
                    if d.chan is None:
                        d.inc = True
        sems = {}
        for e in ENGS:
            c = 0
            for op in self.ops[e]:
                if op.chan is None and op.inc:
                    c += 1
                    op.cnt = c
            sems[e] = stack.enter_context(nc.semaphore('s_' + e))
        csems = {}
        for ch in self.chans:
            csems[ch] = stack.enter_context(nc.semaphore('c_' + ch))
        ops = self.ops
        chans = self.chans

        def body_for(e):
            def body(eng):
                known = {}
                for op in ops[e]:
                    for d in op.rdeps:
                        if d.chan is not None:
                            k = ('c', d.chan)
                            v = 16 * d.cidx
                            sem = csems[d.chan]
                        else:
                            k = d.eng
                            v = d.cnt
                            sem = sems[d.eng]
                        if known.get(k, 0) >= v:
                            continue
                        known[k] = v
                        eng.wait_ge(sem, v)
                    ins = op.fn(eng)
                    if op.chan is not None:
                        ins.then_inc(csems[op.chan], 16)
                    elif op.inc:
                        ins.then_inc(sems[e], 1)
                if e == 'sp':
                    for ch in final_chans:
                        eng.wait_ge(csems[ch], 16 * chans[ch][0])
            return body

        with nc.Block() as block:
            block.sync(body_for('sp'))
            block.tensor(body_for('pe'))
            block.scalar(body_for('act'))
            block.vector(body_for('dve'))
            block.gpsimd(body_for('pool'))


def _pc(w, c):
    n = w.shape[1]
    return np.ascontiguousarray(w.reshape(c, 128, n).transpose(1, 0, 2)).reshape(128, c * n)


def build_weight_tiles(inp):
    tiles = []
    index = {}

    def put(name, arr):
        index[name] = len(tiles)
        tiles.append(arr)

    ffns = [(inp['ffn1_w_in'][0], inp['ffn1_w_out'][0]), (inp['ffn2_w_in'][0], inp['ffn2_w_out'][0]),
            (inp['ffn1_w_in'][1], inp['ffn1_w_out'][1]), (inp['ffn2_w_in'][1], inp['ffn2_w_out'][1])]
    for f, (w_in, w_out) in enumerate(ffns):
        for j in range(NJ):
            gu = np.concatenate([w_in[:, j * 128:(j + 1) * 128], w_in[:, DFF + j * 128:DFF + (j + 1) * 128]], axis=1)
            put(('f', f, 'in', j), _pc(gu, 8))
        for i in range(8):
            put(('f', f, 'out', i), _pc(w_out[:, i * 128:(i + 1) * 128], NJ))
    wqkv = inp['a_w_qkv'][0]
    for a in range(8):
        qk = np.concatenate([wqkv[:, a * 128:(a + 1) * 128], wqkv[:, 1024 + a * 128:1024 + (a + 1) * 128]], axis=1)
        put(('a', 'qk', a), _pc(qk, 8))
        put(('a', 'v', a), _pc(wqkv[:, 2048 + a * 128:2048 + (a + 1) * 128], 8))
    awo = inp['a_w_o'][0]
    for i in range(8):
        put(('a', 'o', i), _pc(awo[:, i * 128:(i + 1) * 128], 8))
    wd = inp['kv_w_down']
    put(('b', 'dckv'), _pc(wd[:, 0:256], 8))
    perm = np.concatenate([np.arange(32, 64), np.arange(0, 32)])
    rope = wd[:, 256:320]
    put(('b', 'drope'), _pc(np.concatenate([rope, rope[:, perm]], axis=1), 8))
    wdq = inp['b_w_dq'][0]
    for m in range(3):
        put(('b', 'dq', m), _pc(wdq[:, m * 256:(m + 1) * 256], 8))
    wup = inp['kv_w_up']
    wuq = inp['b_w_uq'][0]
    for h in range(8):
        wk = _pc(wup[:, h * 256:h * 256 + 128], 2)
        wv = _pc(wup[:, h * 256 + 128:h * 256 + 256], 2)
        wqn = _pc(wuq[:, h * 192:h * 192 + 128], 6)
        qr = wuq[:, h * 192 + 128:h * 192 + 192]
        wqr = _pc(qr, 6)
        wqs = _pc(qr[:, perm], 6)
        put(('b', 'head', h), np.concatenate([wk, wv, wqn, wqr, wqs], axis=1))
    bwo = inp['b_w_o'][0]
    for i in range(8):
        put(('b', 'o', i), _pc(bwo[:, i * 128:(i + 1) * 128], 8))
    blob = np.zeros((len(tiles), 128, WT), np.float32)
    widths = {}
    for name, k in index.items():
        a = tiles[k]
        blob[k, :, :a.shape[1]] = a
        widths[name] = a.shape[1]
    return blob, index, widths


def tile_index():
    index = {}
    widths = {}
    k = 0

    def put(name, w):
        nonlocal k
        index[name] = k
        widths[name] = w
        k += 1
    for f in range(4):
        for j in range(NJ):
            put(('f', f, 'in', j), 2048)
        for i in range(8):
            put(('f', f, 'out', i), 2816)
    for a in range(8):
        put(('a', 'qk', a), 2048)
        put(('a', 'v', a), 1024)
    for i in range(8):
        put(('a', 'o', i), 1024)
    put(('b', 'dckv'), 2048)
    put(('b', 'drope'), 1024)
    for m in range(3):
        put(('b', 'dq', m), 2048)
    for h in range(8):
        put(('b', 'head', h), 2048)
    for i in range(8):
        put(('b', 'o', i), 1024)
    return index, widths, k


def build_consts(inp):
    def g8(v):
        return v.reshape(-1, 128).T
    gains = np.zeros((128, NG), np.float32)
    gains[:, 0:8] = g8(inp['ffn1_norm'][0])
    gains[:, 8:16] = g8(inp['ffn1_norm'][1])
    gains[:, 16:24] = g8(inp['mix_norm'][0])
    gains[:, 24:32] = g8(inp['mix_norm'][1])
    gains[:, 32:40] = g8(inp['ffn2_norm'][0])
    gains[:, 40:48] = g8(inp['ffn2_norm'][1])
    gains[:, 48:56] = g8(inp['kv_norm'])
    gains[:, 56:64] = g8(inp['final_norm'])
    gains[:, 64:66] = g8(inp['kv_latent_norm'])
    gains[:, 66:72] = g8(inp['b_q_norm'][0])
    tab = inp['a_rel_bias'][0]
    cvec = np.ascontiguousarray(np.broadcast_to(tab[:, 256][None, :], (128, 16))).astype(np.float32)
    j = np.arange(128)[:, None, None]
    dl = np.arange(2)[None, :, None]
    i = np.arange(128)[None, None, :]
    idx = np.clip(128 * dl + i - j, -128, 128) + 128
    bias = tab[:, idx]
    bias = bias.reshape(8, 2, 128, 2, 128).transpose(0, 2, 1, 3, 4)
    bias = np.ascontiguousarray(bias).reshape(8, 128, 512).astype(np.float32)
    half = 32
    freqs = (10000.0 ** (-np.arange(half, dtype=np.float32) / half)).astype(np.float32)
    ang = np.arange(S, dtype=np.float32)[:, None] * freqs[None, :]
    cos = np.cos(ang).astype(np.float32).T
    sin = np.sin(ang).astype(np.float32).T
    cosT = np.ascontiguousarray(np.concatenate([cos, cos], axis=0))
    sinT = np.ascontiguousarray(np.concatenate([-sin, sin], axis=0))
    return gains, cvec, bias, cosT, sinT


def build_program(upto=99):
    nc = bass.Bass("TRN2", target_bir_lowering=False)
    windex, wwidth, ntiles = tile_index()
    xT = nc.dram_tensor("xT", [D, T], F32, kind="ExternalInput").ap()
    wblob = nc.dram_tensor("wblob", [ntiles, 128, WT], F32, kind="ExternalInput").ap()
    gains_d = nc.dram_tensor("gains", [128, NG], F32, kind="ExternalInput").ap()
    cvec_d = nc.dram_tensor("cvec", [128, 16], F32, kind="ExternalInput").ap()
    bias_d = nc.dram_tensor("biasA", [8, 128, 512], F32, kind="ExternalInput").ap()
    cos_d = nc.dram_tensor("cosT", [64, S], F32, kind="ExternalInput").ap()
    sin_d = nc.dram_tensor("sinT", [64, S], F32, kind="ExternalInput").ap()
    ident_d = nc.dram_tensor("ident", [128, 128], F32, kind="ExternalInput").ap()
    outT = nc.dram_tensor("outT", [D, T], F32, kind="ExternalOutput").ap()
    xs_d = [nc.dram_tensor("xs%d" % i, [D, T], F32, kind="Internal").ap() for i in range(6)]

    sch = Sched()
    stack = ExitStack()
    with stack:
        def sb_alloc(name, shape, dt):
            return stack.enter_context(nc.sbuf_tensor(name, shape, dt))

        def ps_alloc(name):
            return stack.enter_context(nc.psum_tensor(name, [128, 512], F32))

        xn = [(sb_alloc("xn%d" % i, [128, 8 * 512], F32), R()) for i in range(2)]
        rr = [(sb_alloc("rr%d" % i, [128, 512], F32), R()) for i in range(3)]
        stg = [(sb_alloc("stg%d" % i, [128, WT], F32), R()) for i in range(3)] if CAST != 'dma' else None
        wbf = [(sb_alloc("wbf%d" % i, [128, WT], BF16), R()) for i in range(4)]
        rstd = [(sb_alloc("rstd%d" % i, [128, 512], F32), R()) for i in range(2)]
        sg = [(sb_alloc("sg%d" % i, [128, 512], F32), R()) for i in range(2)]
        gains = sb_alloc("gains_sb", [128, NG], F32)
        gR = R()
        cvec = sb_alloc("cvec_sb", [128, 16], F32)
        cvR = R()
        ones = sb_alloc("ones_sb", [128, 128], BF16)
        onesR = R()
        A16N = 66560 if CAST == 'dma' else 48128
        arena16 = sb_alloc("arena16", [128, A16N], BF16)
        arena32 = sb_alloc("arena32", [128, 2048], F32)
        P = [(ps_alloc("ps%d" % i), R()) for i in range(8)]

        sch.add('sp', lambda e: e.dma_start(out=gains[:], in_=gains_d), wr=[gR], chan='gains')
        sch.add('sp', lambda e: e.dma_start(out=cvec[:], in_=cvec_d), wr=[cvR], chan='cvec')
        sch.add('pool', lambda e: e.memset(ones[:], 1.0), wr=[onesR])
        ones32 = sb_alloc("ones32_sb", [128, 128], F32)
        ones32R = R()
        sch.add('pool', lambda e: e.memset(ones32[:], 1.0), wr=[ones32R])
        ident = sb_alloc("ident_sb", [128, 128], BF16)
        identR = R()
        sch.add('pool', lambda e: e.dma_start(out=ident[:], in_=ident_d), wr=[identR], chan='ident')
        epsc = sb_alloc("eps_sb", [128, 8], F32)
        epsR = R()
        sch.add('pool', lambda e: e.memset(epsc[:], EPS), wr=[epsR])

        cnt = {}

        def nxt(name, n):
            v = cnt.get(name, 0)
            cnt[name] = v + 1
            return v % n

        arena_rs = []
        fence_ops = []

        def new_stage():
            ops = set()
            for r in arena_rs:
                if r.lw is not None:
                    ops.add(r.lw)
                ops.update(r.rd)
            del arena_rs[:]
            best = {}
            for d in ops:
                k = ('c', d.chan) if d.chan is not None else d.eng
                key = d.cidx if d.chan is not None else d.pos
                if k not in best or key > best[k][0]:
                    best[k] = (key, d)
            for d in fence_ops:
                k = ('c', d.chan) if d.chan is not None else d.eng
                key = d.cidx if d.chan is not None else d.pos
                if k not in best or key > best[k][0]:
                    best[k] = (key, d)
            del fence_ops[:]
            fence_ops.extend(v[1] for v in best.values())
            cur16[0] = 0
            cur32[0] = 0

        cur16 = [0]
        cur32 = [0]

        def AR(name=''):
            r = R(name)
            r.rd = list(fence_ops)
            arena_rs.append(r)
            return r

        def a16(n):
            o = cur16[0]
            cur16[0] += n
            assert cur16[0] <= A16N, cur16[0]
            return arena16[:, o:o + n]

        def a32(n):
            o = cur32[0]
            cur32[0] += n
            assert cur32[0] <= 2048
            return arena32[:, o:o + n]

        def MM(out, lhsT, rhs, start, stop, rd, wr):
            sch.add('pe', lambda e: e.matmul(out, lhsT, rhs, start=start, stop=stop), rd, wr)

        def RCP(out, in_, rd, wr, mode=None):
            if (mode or RECIP) == 'act':
                sch.add('act', lambda e: e.activation(out, in_, AF.Ln), rd, wr)
                sch.add('act', lambda e: e.activation(out, out, AF.Exp, scale=-1.0), wr, wr)
            else:
                sch.add('dve', lambda e: e.reciprocal(out, in_), rd, wr)

        def dma(out, in_, rd, wr, chan):
            return sch.add('sp', lambda e: e.dma_start(out=out, in_=in_), rd, wr, chan)

        steps = []

        def step(tiles, fn):
            steps.append((tiles, fn))

        def issue_tile(name, k):
            w = wwidth[name]
            si = k % 3
            bi = k % 4
            wb, wbR = wbf[bi]
            src = wblob[windex[name], :, 0:w]
            if CAST != 'dma':
                st, stR = stg[si]
            if CAST == 'dma':
                sch.add('pool', lambda e: e.dma_start(out=wb[:, 0:w], in_=src), [], [wbR], 'wq%d' % bi)
                return (wb, wbR)
            dma(st[:, 0:w], src, [], [stR], 'stg%d' % si)
            ce = CAST if isinstance(CAST, str) else CAST[k % len(CAST)]
            if ce == 'act':
                sch.add('act', lambda e: e.activation(wb[:, 0:w], st[:, 0:w], AF.Copy), [stR], [wbR])
            else:
                sch.add(ce, lambda e: e.tensor_copy(wb[:, 0:w], st[:, 0:w]), [stR], [wbR])
            return (wb, wbR)

        def dram_rs():
            return [[R() for _ in range(8)] for _ in range(NSB)]

        def xview(x, i, sb):
            return x[i * 128:(i + 1) * 128, sb * 512:(sb + 1) * 512]

        def norm_a(x3, xR, C, sq):
            if isinstance(sq, list):
                sq3, sqR = sq[nxt('sq', len(sq))]
            else:
                sq3, sqR = sq
            if SQ_ENG == 'act':
                sch.add('act', lambda e: e.activation(sq3[:, 0:C, :], x3, AF.Square), [xR], [sqR])
            else:
                sch.add(SQ_ENG, lambda e: e.tensor_tensor(sq3[:, 0:C, :], x3, x3, ALU.mult), [xR], [sqR])
            return sq3, sqR

        def norm_core(x3, xR, C, gcol, out3, outR, n, sq):
            sq3, sqR = norm_a(x3, xR, C, sq)
            norm_b(x3, xR, C, gcol, out3, outR, n, sq3, sqR)

        def norm_b(x3, xR, C, gcol, out3, outR, n, sq3, sqR):
            ps, psR = P[7]
            for c in range(C):
                MM(ps[:], ones[:], sq3[:, c, :], c == 0, c == C - 1, [sqR, onesR], [psR])
            k = nxt('rstd', 2)
            rs, rsR = rstd[k]
            if RSTD == 'lnexp':
                sch.add('act', lambda e: e.activation(rs[:], ps[:], AF.Ln, bias=epsc[:, 0:1], scale=1.0 / n),
                        [psR, epsR], [rsR])
                sch.add('act', lambda e: e.activation(rs[:], rs[:], AF.Exp, scale=-0.5), [rsR], [rsR])
            else:
                sch.add('act', lambda e: e.activation(rs[:], ps[:], AF.Sqrt, bias=epsc[:, 0:1], scale=1.0 / n),
                        [psR, epsR], [rsR])
                sch.add('dve', lambda e: e.reciprocal(rs[:], rs[:]), [rsR], [rsR])
            for c in range(C):
                sch.add('dve', (lambda c: lambda e: e.scalar_tensor_tensor(
                    out3[:, c, :], x3[:, c, :], gains[:, gcol + c:gcol + c + 1], rs[:], ALU.mult, ALU.mult))(c),
                    [xR, rsR, gR], [outR])

        def load_norm(src, srcR, sb, gcol, out3, outR, sq):
            k = nxt('xn', 2)
            xs, xR = xn[k]
            x3 = xs[:].rearrange('p (c t) -> p c t', c=8)
            dma(x3, src.rearrange('(c p) t -> p c t', p=128)[:, :, sb * 512:(sb + 1) * 512],
                list(srcR[sb]), [xR], 'xn%d' % k)
            norm_core(x3, xR, 8, gcol, out3, outR, float(D), sq)

        def load_norm_seq(src, srcR, t0, gcol, out_fn, outRs, sq):
            hold = {}

            def p1(sbl):
                k = nxt('xn', 2)
                xs, xR = xn[k]
                x3 = xs[:].rearrange('p (c t) -> p c t', c=8)
                dma(x3, src.rearrange('(c p) t -> p c t', p=128)[:, :, (t0 + sbl) * 512:(t0 + sbl + 1) * 512],
                    list(srcR[t0 + sbl]), [xR], 'xn%d' % k)
                sq3, sqR = norm_a(x3, xR, 8, sq)
                hold[sbl] = (x3, xR, sq3, sqR)

            def p2(sbl):
                x3, xR, sq3, sqR = hold[sbl]
                norm_b(x3, xR, 8, gcol, out_fn(sbl), outRs[sbl], float(D), sq3, sqR)
            p1(0)
            p1(1)
            p2(0)
            p1(2)
            p2(1)
            p1(3)
            p2(2)
            p2(3)

        def resid(po, poR, src, srcR, dst, dstR, i, sb, scale):
            k = nxt('rr', 3)
            r, rR = rr[k]
            dma(r[:], xview(src, i, sb), [srcR[sb][i]], [rR], 'rl%d' % k)
            sch.add('dve', lambda e: e.scalar_tensor_tensor(r[:], po, scale, r[:], ALU.mult, ALU.add),
                    [poR, rR], [rR])
            dma(xview(dst, i, sb), r[:], [rR], [dstR[sb][i]], 'rs%d' % k)

        def ffn_stage(f, src, srcR, dst, dstR, gcol, final):
            st = {}
            NSF = 4 if CAST == 'dma' else 2
            TB = NSF * 512

            def setup(_w):
                new_stage()
                st['sq'] = (a16(4096).rearrange('p (c t) -> p c t', c=8), AR())
                st['hn'] = a16(8 * TB).rearrange('p (c t) -> p c t', c=8)
                st['hnR'] = [AR() for _ in range(NSF)]
                st['act'] = a16(NJ * TB).rearrange('p (j t) -> p j t', j=NJ)
                st['actR'] = [AR() for _ in range(NSF)]
            step([], setup)

            def norm_fn(b):
                def fn(_w):
                    for s in range(NSF):
                        load_norm(src, srcR, b * NSF + s, gcol, st['hn'][:, :, s * 512:(s + 1) * 512],
                                  st['hnR'][s], st['sq'])
                return fn

            def up_fn(b, j):
                def fn(w):
                    wb, wR = w[0]
                    w3 = wb[:, 0:2048].rearrange('p (c n) -> p c n', c=8)
                    hn = st['hn']
                    for s in range(NSF):
                        k = nxt('gu', 2)
                        pg, pgR = P[2 * k]
                        pu, puR = P[2 * k + 1]
                        for c in range(8):
                            MM(pg[:], w3[:, c, 0:128], hn[:, c, s * 512:(s + 1) * 512], c == 0, c == 7,
                               [wR, st['hnR'][s]], [pgR])
                        for c in range(8):
                            MM(pu[:], w3[:, c, 128:256], hn[:, c, s * 512:(s + 1) * 512], c == 0, c == 7,
                               [wR, st['hnR'][s]], [puR])
                        sgt, sgR = sg[k]
                        sch.add('act', lambda e, sgt=sgt, pg=pg: e.activation(sgt[:], pg[:], AF.Silu), [pgR], [sgR])
                        av = st['act'][:, j, s * 512:(s + 1) * 512]
                        sch.add('dve', lambda e, av=av, sgt=sgt, pu=pu: e.tensor_tensor(av, sgt[:], pu[:], ALU.mult),
                                [sgR, puR], [st['actR'][s]])
                return fn

            def down_fn(b, i):
                def fn(w):
                    wb, wR = w[0]
                    w3 = wb[:, 0:2816].rearrange('p (j n) -> p j n', j=NJ)
                    for s in range(NSF):
                        sb = b * NSF + s
                        k = nxt('ob', 3)
                        po, poR = P[4 + k]
                        for j in range(NJ):
                            MM(po[:], w3[:, j, :], st['act'][:, j, s * 512:(s + 1) * 512], j == 0, j == NJ - 1,
                               [wR, st['actR'][s]], [poR])
                        resid(po[:], poR, src, srcR, dst, dstR, i, sb, 0.5)
                return fn

            def final_fn(b):
                def fn(_w):
                    for s in range(NSF):
                        sb = b * NSF + s
                        k = nxt('xn', 2)
                        xs, xR = xn[k]
                        x3 = xs[:].rearrange('p (c t) -> p c t', c=8)
                        dma(x3, dst.rearrange('(c p) t -> p c t', p=128)[:, :, sb * 512:(sb + 1) * 512],
                            list(dstR[sb]), [xR], 'xn%d' % k)
                        norm_core(x3, xR, 8, G_FINAL, x3, xR, float(D), st['sq'])
                        dma(outT.rearrange('(c p) t -> p c t', p=128)[:, :, sb * 512:(sb + 1) * 512], x3,
                            [xR], list(outR[sb]), 'fo%d' % k)
                return fn

            NB = T // TB
            slots = {}

            def ld_fn(b, ss):
                def fn(_w):
                    for s_ in ss:
                        sb = b * NSF + s_
                        k = nxt('xn', 2)
                        xs, xR = xn[k]
                        x3 = xs[:].rearrange('p (c t) -> p c t', c=8)
                        dma(x3, src.rearrange('(c p) t -> p c t', p=128)[:, :, sb * 512:(sb + 1) * 512],
                            list(srcR[sb]), [xR], 'xn%d' % k)
                        slots[(b, s_)] = k
                return fn

            def nm_fn(b, s_):
                def fn(_w):
                    xs, xR = xn[slots[(b, s_)]]
                    x3 = xs[:].rearrange('p (c t) -> p c t', c=8)
                    norm_core(x3, xR, 8, gcol, st['hn'][:, :, s_ * 512:(s_ + 1) * 512], st['hnR'][s_],
                              float(D), st['sq'])
                return fn

            if NSF != 4:
                step([], norm_fn(0))
            else:
                step([], ld_fn(0, [0, 1]))
                step([], nm_fn(0, 0))
                step([], ld_fn(0, [2]))
                step([], nm_fn(0, 1))
                step([], ld_fn(0, [3]))
                step([], nm_fn(0, 2))
                step([], nm_fn(0, 3))
            for b in range(NB):
                for j in range(NJ):
                    step([('f', f, 'in', j)], up_fn(b, j))
                    if final and b > 0 and j == 5:
                        step([], final_fn(b - 1))
                for i in range(8):
                    nb_ = b + 1 < NB
                    if NSF == 4 and nb_ and i == 0:
                        step([], ld_fn(b + 1, [0, 1]))
                    step([('f', f, 'out', i)], down_fn(b, i))
                    if NSF != 4:
                        if i == 3 and nb_:
                            step([], norm_fn(b + 1))
                    elif nb_:
                        if i == 2:
                            step([], nm_fn(b + 1, 0))
                            step([], ld_fn(b + 1, [2]))
                        elif i == 3:
                            step([], nm_fn(b + 1, 1))
                            step([], ld_fn(b + 1, [3]))
                        elif i == 4:
                            step([], nm_fn(b + 1, 2))
                        elif i == 5:
                            step([], nm_fn(b + 1, 3))
            if final:
                step([], final_fn(NB - 1))

        def wo_load(st, i, src, srcR, t0):
            k = nxt('xn', 2)
            xs, xR = xn[k]
            dma(xs[:, 0:2048], src[i * 128:(i + 1) * 128, t0 * 512:(t0 + 4) * 512],
                [srcR[t0 + sbl][i] for sbl in range(4)], [xR], 'xn%d' % k)
            st['wo_slot'][i] = k

        def wo_fn(st, i, src, srcR, dst, dstR, t0):
            def fn(w):
                wb, wR = w[0]
                w3 = wb[:, 0:1024].rearrange('p (a n) -> p a n', a=8)
                ao = st['ao']
                if i == 0:
                    st['wo_slot'] = {}
                    wo_load(st, 0, src, srcR, t0)
                if i + 1 < 8:
                    wo_load(st, i + 1, src, srcR, t0)
                k = st['wo_slot'][i]
                xs, xR = xn[k]
                for sbl in range(4):
                    kk = nxt('pj', 2)
                    po, poR = P[kk]
                    for a in range(8):
                        MM(po[:], w3[:, a, :], ao[:, a, sbl * 512:(sbl + 1) * 512], a == 0, a == 7,
                           [wR, st['aoR'][sbl]], [poR])
                    r = xs[:, sbl * 512:(sbl + 1) * 512]
                    sch.add('dve', lambda e, r=r, po=po: e.tensor_tensor(r, po[:], r, ALU.add), [poR, xR], [xR])
                dma(dst[i * 128:(i + 1) * 128, t0 * 512:(t0 + 4) * 512], xs[:, 0:2048],
                    [xR], [dstR[t0 + sbl][i] for sbl in range(4)], 'xs%d' % k)
            return fn

        def a_stage(seq, src, srcR, dst, dstR):
            st = {}
            t0 = seq * 4

            def setup(_w):
                new_stage()
                st['sq'] = [(a16(4096).rearrange('p (c t) -> p c t', c=8), AR()) for _ in range(2)]
                st['hn'] = a16(8 * S).rearrange('p (c t) -> p c t', c=8)
                st['hnR'] = [AR() for _ in range(4)]
                st['ao'] = a16(8 * S).rearrange('p (c t) -> p c t', c=8)
                st['aoR'] = [AR() for _ in range(4)]
                st['qkv'] = []
                for _ in range(2):
                    qb_ = (a16(2 * S).rearrange('p (h t) -> p h t', h=2), AR())
                    kb_ = (a16(S), AR())
                    vb_ = (a16(S).rearrange('p (t n) -> p t n', n=128), AR())
                    st['qkv'].append((qb_, kb_, vb_))
                    q2_, q2R_ = qb_
                    sch.add('pool', (lambda q2_: lambda e: e.memset(q2_[64:128, 0, :], 0.0))(q2_), [], [q2R_])
                    sch.add('pool', (lambda q2_: lambda e: e.memset(q2_[0:64, 1, :], 0.0))(q2_), [], [q2R_])
                st['pT'] = [(a16(512), AR()) for _ in range(3)]
                st['bias'] = [(a32(512), AR()) for _ in range(2)]
                st['tmp'] = [(a32(256), AR()) for _ in range(3)]
                st['btb'] = [(a16(512), AR()) for _ in range(2)]
                load_norm_seq(src, srcR, t0, G_MIX[0], lambda sbl: st['hn'][:, :, sbl * 512:(sbl + 1) * 512],
                              st['hnR'], st['sq'])
            step([], setup)

            def proj_groups(a, w):
                wqk, wqkR = w[0]
                wv, wvR = w[1]
                wqk3 = wqk[:, 0:2048].rearrange('p (c n) -> p c n', c=8)
                wv3 = wv[:, 0:1024].rearrange('p (c n) -> p c n', c=8)
                hn = st['hn']
                (qT, qR), (kT, kR), (v3, vR) = st['qkv'][a % 2]
                pp, ppR = P[1]
                groups = []
                for sbl in range(4):
                    cs = slice(sbl * 512, (sbl + 1) * 512)
                    hR = st['hnR'][sbl]

                    def gq(cs=cs, hR=hR):
                        for c in range(8):
                            MM(pp[:], wqk3[:, c, 0:128], hn[:, c, cs], c == 0, c == 7, [wqkR, hR], [ppR])
                        sch.add('dve', lambda e: e.tensor_scalar(
                            qT[0:64, 0, cs], pp[0:64, :], 0.125, None, ALU.mult), [ppR], [qR])
                        sch.add('dve', lambda e: e.tensor_scalar(
                            qT[64:128, 1, cs], pp[64:128, :], 0.125, None, ALU.mult), [ppR], [qR])

                    def gk(cs=cs, hR=hR):
                        for c in range(8):
                            MM(pp[:], wqk3[:, c, 128:256], hn[:, c, cs], c == 0, c == 7, [wqkR, hR], [ppR])
                        sch.add('dve', lambda e: e.tensor_copy(kT[:, cs], pp[:]), [ppR], [kR])

                    def gv(sbl=sbl, hR=hR):
                        for tt in range(4):
                            tok = slice(sbl * 512 + tt * 128, sbl * 512 + (tt + 1) * 128)
                            for c in range(8):
                                MM(pp[:, tt * 128:(tt + 1) * 128], hn[:, c, tok], wv3[:, c, :], c == 0, c == 7,
                                   [wvR, hR], [ppR])
                        sch.add('dve', lambda e: e.tensor_copy(
                            v3[:, sbl * 4:(sbl + 1) * 4, :], pp[:].rearrange('p (t n) -> p t n', n=128)),
                            [ppR], [vR])
                    groups += [gq, gk, gv]
                return groups

            def prologue_fn(w):
                for g in proj_groups(0, w):
                    g()
            step([('a', 'qk', 0), ('a', 'v', 0)], prologue_fn)

            def pair_fn(a):
                def fn(w):
                    nxt_groups = proj_groups(a + 1, w) if a + 1 < 8 else []
                    (qT, qR), (kT, kR), (v3, vR) = st['qkv'][a % 2]
                    bk = nxt('bias', 2)
                    bt, btR = st['bias'][bk]
                    dma(bt, bias_d[a], [], [btR], 'bias%d' % bk)
                    btb, btbR = st['btb'][bk]
                    for hh_ in range(2):
                        sch.add('dve', (lambda hh_: lambda e: e.tensor_scalar(
                            btb[:, hh_ * 256:(hh_ + 1) * 256], bt[:, hh_ * 256:(hh_ + 1) * 256],
                            cvec[:, 2 * a + hh_:2 * a + hh_ + 1], None, ALU.subtract))(hh_), [btR, cvR], [btbR])
                    units = []
                    for QB in range(4):
                        for hh in range(2):
                            rs_ = [r for r in (3, 4, 0, 1, 2, 5, 6, 7) if 4 * QB - 4 + r >= 0]
                            for n_, r in enumerate(rs_):
                                units.append((QB, hh, r, n_ == 0, n_ == len(rs_) - 1))
                    SB3 = [P[3], P[4], P[5]]

                    def rng(r):
                        return (0, (r + 1) * 128) if r <= 3 else ((r - 4) * 128, 512)

                    def scores(u):
                        QB, hh, r, first, last = units[u]
                        kt = 4 * QB - 4 + r
                        c0, c1 = rng(r)
                        kb = u % 3
                        sp_, spR = SB3[kb]
                        pT, pTR = st['pT'][kb]
                        tmp, tmpR = st['tmp'][kb]
                        h = 2 * a + hh
                        has_g = r >= 3
                        MM(sp_[:, c0:c1], kT[:, kt * 128:(kt + 1) * 128], qT[:, hh, QB * 512 + c0:QB * 512 + c1],
                           True, not has_g, [kR, qR], [spR])
                        if r >= 4:
                            g0, g1 = c0, min(c0 + 256, 512)
                            b0 = hh * 256
                            k0, k1 = g1, 512
                        elif r == 3:
                            g0, g1 = 0, 128
                            b0 = hh * 256 + 128
                            k0, k1 = 128, c1
                        else:
                            g0 = g1 = 0
                            b0 = 0
                            k0, k1 = 0, c1
                        if g1 > g0:
                            MM(sp_[:, g0:g1], ident[:], btb[:, b0:b0 + (g1 - g0)], False, True,
                               [identR, btbR], [spR])
                        sch.add('act', lambda e: e.activation(
                            pT[:, c0:c1], sp_[:, c0:c1], AF.Exp, bias=cvec[:, h:h + 1]), [spR, cvR], [pTR])
                        if r >= 4:
                            sch.add('pool', lambda e: e.memset(pT[64:128, c0:c0 + 64], 0.0), [], [pTR])
                        if r <= 3:
                            sch.add('pool', lambda e: e.memset(pT[0:64, c1 - 64:c1], 0.0), [], [pTR])

                    def pv_(u):
                        QB, hh, r, first, last = units[u]
                        kt = 4 * QB - 4 + r
                        c0, c1 = rng(r)
                        pl = slice(hh * 64, hh * 64 + 64)
                        kb = u % 3
                        pT, pTR = st['pT'][kb]
                        ob = (2 * QB + hh) % 2
                        po, poR = P[2] if ob == 0 else P[6]
                        pd, pdR = P[7] if ob == 0 else P[0]
                        MM(po[:, c0:c1], v3[:, kt, :], pT[:, c0:c1], first, last, [vR, pTR], [poR])
                        MM(pd[:, c0:c1], ones[:], pT[:, c0:c1], first, last, [onesR, pTR], [pdR])
                        if last:
                            rk = nxt('rstd', 2)
                            rd_, rdR = rstd[rk]
                            RCP(rd_[pl, :], pd[pl, :], [pdR], [rdR], 'dve' if hh == 0 else 'act')
                            ao = st['ao']
                            sch.add('dve', lambda e: e.tensor_tensor(
                                ao[pl, a, QB * 512:(QB + 1) * 512], po[pl, :], rd_[pl, :], ALU.mult),
                                [poR, rdR], [st['aoR'][QB]])

                    scores(0)
                    scores(1)
                    gi = 0
                    for u in range(len(units)):
                        if u + 2 < len(units):
                            scores(u + 2)
                        pv_(u)
                        want = (len(nxt_groups) * (u + 1)) // len(units)
                        while gi < want:
                            nxt_groups[gi]()
                            gi += 1
                return fn

            for a in range(8):
                step([('a', 'qk', a + 1), ('a', 'v', a + 1)] if a + 1 < 8 else [], pair_fn(a))
            for i in range(8):
                step([('a', 'o', i)], wo_fn(st, i, src, srcR, dst, dstR, t0))

        def b_stage(seq, kvsrc, kvsrcR, src, srcR, dst, dstR):
            st = {}
            t0 = seq * 4
            scale = float(192 ** -0.5)

            def load_tab(sbl):
                tk = nxt('tab', 2)
                tb, tbR = st['tab'][tk]
                dma(tb[0:64, 0:512], cos_d[:, sbl * 512:(sbl + 1) * 512], [], [tbR], 'tab%d' % tk)
                dma(tb[0:64, 512:1024], sin_d[:, sbl * 512:(sbl + 1) * 512], [], [tbR], 'tab%d' % tk)
                return tb, tbR

            def rope_out(pr, prR, psw, pswR, tb, tbR, out_ap, outR):
                if nxt('ropet', 2) == 0:
                    (t1, t1R), (t2, t2R) = sg[0], sg[1]
                else:
                    (t1, t1R), (t2, t2R) = rstd[0], rstd[1]
                sch.add('dve', lambda e: e.tensor_tensor(t1[0:64, :], pr[0:64, :], tb[0:64, 0:512], ALU.mult),
                        [prR, tbR], [t1R])
                sch.add('dve', lambda e: e.tensor_tensor(t2[0:64, :], psw[0:64, :], tb[0:64, 512:1024], ALU.mult),
                        [pswR, tbR], [t2R])
                sch.add('dve', lambda e: e.tensor_tensor(out_ap, t1[0:64, :], t2[0:64, :], ALU.add),
                        [t1R, t2R], [outR])

            def setup(_w):
                new_stage()
                st['sq'] = [(a16(4096).rearrange('p (c t) -> p c t', c=8), AR()) for _ in range(2)]
                st['ao'] = a16(8 * S).rearrange('p (c t) -> p c t', c=8)
                st['aoR'] = [AR() for _ in range(4)]
                st['cq'] = a16(6 * S).rearrange('p (c t) -> p c t', c=6)
                st['cqR'] = [AR() for _ in range(4)]
                st['ckv'] = a16(2 * S).rearrange('p (c t) -> p c t', c=2)
                st['ckvR'] = [AR() for _ in range(4)]
                st['kr'] = (a16(S), AR())
                st['kn'] = (a16(S), AR())
                st['vh'] = (a16(S).rearrange('p (t n) -> p t n', n=128), AR())
                st['qn'] = (a16(S), AR())
                st['qr'] = (a16(S), AR())
                st['pT'] = [(a16(512), AR()) for _ in range(3)]
                for nm in ('kr', 'qr'):
                    buf, bR = st[nm]
                    sch.add('pool', (lambda buf: lambda e: e.memset(buf[64:128, :], 0.0))(buf), [], [bR])
                st['tab'] = [(a32(1024), AR()) for _ in range(2)]
                load_norm_seq(kvsrc, kvsrcR, t0, G_KV, lambda sbl: st['ao'][:, :, sbl * 512:(sbl + 1) * 512],
                              st['aoR'], st['sq'])
            step([], setup)

            def ckv_fn(w):
                wb, wR = w[0]
                w3 = wb[:, 0:2048].rearrange('p (c n) -> p c n', c=8)
                hn = st['ao']
                for sbl in range(4):
                    cs = slice(sbl * 512, (sbl + 1) * 512)
                    xk = nxt('xn', 2)
                    xs, xR = xn[xk]
                    x3 = xs[:].rearrange('p (c t) -> p c t', c=8)
                    for m in range(2):
                        k = nxt('pj', 2)
                        pp, ppR = P[k]
                        for c in range(8):
                            MM(pp[:], w3[:, c, m * 128:(m + 1) * 128], hn[:, c, cs], c == 0, c == 7,
                               [wR, st['aoR'][sbl]], [ppR])
                        sch.add('dve', lambda e, pp=pp, m=m, x3=x3: e.tensor_copy(x3[:, m, :], pp[:]), [ppR], [xR])
                    norm_core(x3[:, 0:2, :], xR, 2, G_LAT, st['ckv'][:, :, cs], st['ckvR'][sbl], 256.0, st['sq'])
            step([('b', 'dckv')], ckv_fn)

            def krope_fn(w):
                wb, wR = w[0]
                w3 = wb[:, 0:1024].rearrange('p (c n) -> p c n', c=8)
                hn = st['ao']
                kr, krR = st['kr']
                for sbl in range(4):
                    cs = slice(sbl * 512, (sbl + 1) * 512)
                    tb, tbR = load_tab(sbl)
                    pr, prR = P[2]
                    psw, pswR = P[3]
                    for c in range(8):
                        MM(pr[0:64, :], w3[:, c, 0:64], hn[:, c, cs], c == 0, c == 7, [wR, st['aoR'][sbl]], [prR])
                    for c in range(8):
                        MM(psw[0:64, :], w3[:, c, 64:128], hn[:, c, cs], c == 0, c == 7, [wR, st['aoR'][sbl]], [pswR])
                    rope_out(pr, prR, psw, pswR, tb, tbR, kr[0:64, cs], krR)
            step([('b', 'drope')], krope_fn)

            def qnorm_fn(_w):
                load_norm_seq(src, srcR, t0, G_MIX[1], lambda sbl: st['ao'][:, :, sbl * 512:(sbl + 1) * 512],
                              st['aoR'], st['sq'])
            step([], qnorm_fn)

            def dq_fn(w):
                hn = st['ao']
                for sbl in range(4):
                    cs = slice(sbl * 512, (sbl + 1) * 512)
                    xk = nxt('xn', 2)
                    xs, xR = xn[xk]
                    x3 = xs[:].rearrange('p (c t) -> p c t', c=8)
                    for m in range(6):
                        wb, wR = w[m // 2]
                        w3 = wb[:, 0:2048].rearrange('p (c n) -> p c n', c=8)
                        k = nxt('pj', 2)
                        pp, ppR = P[k]
                        for c in range(8):
                            MM(pp[:], w3[:, c, (m % 2) * 128:(m % 2 + 1) * 128], hn[:, c, cs], c == 0, c == 7,
                               [wR, st['aoR'][sbl]], [ppR])
                        sch.add('dve', lambda e, pp=pp, m=m, x3=x3: e.tensor_copy(x3[:, m, :], pp[:]), [ppR], [xR])
                    norm_core(x3[:, 0:6, :], xR, 6, G_QN, st['cq'][:, :, cs], st['cqR'][sbl], 768.0, st['sq'])
            step([('b', 'dq', 0), ('b', 'dq', 1), ('b', 'dq', 2)], dq_fn)

            def head_fn(h):
                def fn(w):
                    wb, wR = w[0]
                    wk = wb[:, 0:256].rearrange('p (c n) -> p c n', c=2)
                    wv = wb[:, 256:512].rearrange('p (c n) -> p c n', c=2)
                    wqn = wb[:, 512:1280].rearrange('p (c n) -> p c n', c=6)
                    wqr = wb[:, 1280:1664].rearrange('p (c n) -> p c n', c=6)
                    wqs = wb[:, 1664:2048].rearrange('p (c n) -> p c n', c=6)
                    ckv = st['ckv']
                    cq = st['cq']
                    kn, knR = st['kn']
                    vh, vhR = st['vh']
                    qn, qnR = st['qn']
                    qr, qrR = st['qr']
                    kr, krR = st['kr']
                    for sbl in range(4):
                        cs = slice(sbl * 512, (sbl + 1) * 512)
                        cR = st['ckvR'][sbl]
                        qR_ = st['cqR'][sbl]
                        k = nxt('pj', 2)
                        pp, ppR = P[k]
                        for c in range(2):
                            MM(pp[:], wk[:, c, :], ckv[:, c, cs], c == 0, c == 1, [wR, cR], [ppR])
                        sch.add('act', lambda e, pp=pp, cs=cs: e.activation(kn[:, cs], pp[:], AF.Copy), [ppR], [knR])
                        k = nxt('pj', 2)
                        pp, ppR = P[k]
                        for tt in range(4):
                            tok = slice(sbl * 512 + tt * 128, sbl * 512 + (tt + 1) * 128)
                            for c in range(2):
                                MM(pp[:, tt * 128:(tt + 1) * 128], ckv[:, c, tok], wv[:, c, :], c == 0, c == 1,
                                   [wR, cR], [ppR])
                        sch.add('act', lambda e, pp=pp, sbl=sbl: e.activation(
                            vh[:, sbl * 4:(sbl + 1) * 4, :], pp[:].rearrange('p (t n) -> p t n', n=128), AF.Copy),
                            [ppR], [vhR])
                        k = nxt('pj', 2)
                        pp, ppR = P[k]
                        for c in range(6):
                            MM(pp[:], wqn[:, c, :], cq[:, c, cs], c == 0, c == 5, [wR, qR_], [ppR])
                        sch.add('act', lambda e, pp=pp, cs=cs: e.activation(qn[:, cs], pp[:], AF.Copy), [ppR], [qnR])
                        tb, tbR = load_tab(sbl)
                        pr, prR = P[2] if sbl % 2 == 0 else P[6]
                        psw, pswR = P[3] if sbl % 2 == 0 else P[7]
                        for c in range(6):
                            MM(pr[0:64, :], wqr[:, c, :], cq[:, c, cs], c == 0, c == 5, [wR, qR_], [prR])
                        for c in range(6):
                            MM(psw[0:64, :], wqs[:, c, :], cq[:, c, cs], c == 0, c == 5, [wR, qR_], [pswR])
                        rope_out(pr, prR, psw, pswR, tb, tbR, qr[0:64, cs], qrR)
                    units = [(Q, kt) for Q in range(4) for kt in range(4 * Q + 4)]

                    def scores(u):
                        Q, kt = units[u]
                        c0 = max(kt - 4 * Q, 0) * 128
                        kb = u % 3
                        sp_, spR = (P[4], P[5], P[0])[kb]
                        qs = slice(Q * 512 + c0, (Q + 1) * 512)
                        MM(sp_[:, c0:512], kn[:, kt * 128:(kt + 1) * 128], qn[:, qs], True, False,
                           [knR, qnR], [spR])
                        MM(sp_[:, c0:512], kr[:, kt * 128:(kt + 1) * 128], qr[:, qs], False, True,
                           [krR, qrR], [spR])
                        pT, pTR = st['pT'][kb]
                        sch.add('act', lambda e: e.activation(pT[:, c0:512], sp_[:, c0:512], AF.Exp, scale=scale),
                                [spR], [pTR])
                        if kt >= 4 * Q:
                            sch.add('pool', lambda e: e.memset(pT[64:128, c0:c0 + 64], 0.0), [], [pTR])

                    def pv_(u):
                        Q, kt = units[u]
                        c0 = max(kt - 4 * Q, 0) * 128
                        kb = u % 3
                        pT, pTR = st['pT'][kb]
                        nkt = 4 * Q + 4
                        po, poR = P[6] if Q % 2 == 0 else P[2]
                        pd, pdR = P[7] if Q % 2 == 0 else P[3]
                        MM(po[:, c0:512], vh[:, kt, :], pT[:, c0:512], kt == 0, kt == nkt - 1, [vhR, pTR], [poR])
                        MM(pd[:, c0:512], ones[:], pT[:, c0:512], kt == 0, kt == nkt - 1, [onesR, pTR], [pdR])
                        if kt == nkt - 1:
                            rk = nxt('rstd', 2)
                            rd_, rdR = rstd[rk]
                            RCP(rd_[:], pd[:], [pdR], [rdR])
                            ao = st['ao']
                            sch.add('dve', lambda e: e.tensor_tensor(
                                ao[:, h, Q * 512:(Q + 1) * 512], po[:], rd_[:], ALU.mult),
                                [poR, rdR], [st['aoR'][Q]])

                    scores(0)
                    scores(1)
                    for u in range(len(units)):
                        if u + 2 < len(units):
                            scores(u + 2)
                        pv_(u)
                return fn

            for h in range(8):
                step([('b', 'head', h)], head_fn(h))
            for i in range(8):
                step([('b', 'o', i)], wo_fn(st, i, src, srcR, dst, dstR, t0))

        xR_in = dram_rs()
        xsR = [dram_rs() for _ in range(6)]
        outR = dram_rs()
        stages = []
        stages.append(lambda dst, dR: ffn_stage(0, xT, xR_in, dst, dR, G_FFN1[0], False))
        stages.append(lambda dst, dR: [a_stage(q, xs_d[0], xsR[0], dst, dR) for q in range(NSEQ)])
        stages.append(lambda dst, dR: ffn_stage(1, xs_d[1], xsR[1], dst, dR, G_FFN2[0], False))
        stages.append(lambda dst, dR: ffn_stage(2, xs_d[2], xsR[2], dst, dR, G_FFN1[1], False))
        stages.append(lambda dst, dR: [b_stage(q, xs_d[2], xsR[2], xs_d[3], xsR[3], dst, dR) for q in range(NSEQ)])
        stages.append(lambda dst, dR: ffn_stage(3, xs_d[4], xsR[4], xs_d[5], xsR[5], G_FFN2[1], True))
        nst = min(upto, len(stages))
        for k in range(nst):
            if k == nst - 1:
                stages[k](outT, outR)
            else:
                stages[k](xs_d[k], xsR[k])

        tile_seq = []
        first_tile = []
        for tiles, fn in steps:
            first_tile.append(len(tile_seq))
            tile_seq.extend(tiles)
        issued = {}
        nissued = 0
        for m, (tiles, fn) in enumerate(steps):
            a = first_tile[m]
            lim = min(len(tile_seq), a + 4)
            while nissued < lim:
                issued[nissued] = issue_tile(tile_seq[nissued], nissued)
                nissued += 1
            fn([issued[a + k] for k in range(len(tiles))])

        final_chans = [ch for ch in sch.chans if ch.startswith('rs') or ch.startswith('fo') or ch.startswith('xs')]
        sch.emit(nc, stack, final_chans)
        stats = {e: len(sch.ops[e]) for e in ENGS}
        stats['maxcnt'] = {e: max([op.cnt for op in sch.ops[e]] + [0]) for e in ENGS}
        stats['maxchan'] = max(v[0] for v in sch.chans.values())
        stats['nsem'] = len(sch.chans) + 5
    return nc, stats


def make_in_maps(inputs):
    inp = {k: np.asarray(v, dtype=np.float32) for k, v in inputs.items()}
    blob, index, widths = build_weight_tiles(inp)
    i2, w2, nt = tile_index()
    assert i2 == index and w2 == widths and nt == blob.shape[0]
    gains, cvec, bias, cosT, sinT = build_consts(inp)
    x = inp['x']
    in_maps = []
    for c in range(NCORES):
        xc = np.ascontiguousarray(x[c * NSEQ:(c + 1) * NSEQ].reshape(T, D).T)
        in_maps.append({"xT": xc, "wblob": blob, "gains": gains, "cvec": cvec, "biasA": bias,
                        "cosT": cosT, "sinT": sinT, "ident": np.eye(128, dtype=np.float32)})
    return in_maps


def kernel(**inputs):
    in_maps = make_in_maps(inputs)
    nc, _ = build_program()
    res = run_bass_kernel_spmd(nc, in_maps, core_ids=list(range(NCORES)))
    outs = []
    for c in range(NCORES):
        o = np.asarray(res.results[c]["outT"], dtype=np.float32)
        outs.append(o.T.reshape(NSEQ, S, D))
    return np.concatenate(outs, axis=0).astype(np.float32)
```

```python
import numpy as np
from contextlib import ExitStack
import concourse.bass as bass
import concourse.mybir as mybir
from concourse.bass_utils import run_bass_kernel_spmd

F32 = mybir.dt.float32
BF16 = mybir.dt.bfloat16
AF = mybir.ActivationFunctionType
ALU = mybir.AluOpType

NCORES = 8
D = 1024
S = 2048
NSEQ = 2
T = NSEQ * S
NSB = T // 512
DFF = 2816
NJ = 22
WT = 2816
EPS = 1e-6
ENGS = ['sp', 'pe', 'act', 'dve', 'pool']

G_FFN1 = [0, 8]
G_MIX = [16, 24]
G_FFN2 = [32, 40]
G_KV = 48
G_FINAL = 56
G_LAT = 64
G_QN = 66
NG = 80
CAST = 'dma'
SQ_ENG = 'act'
RSTD = 'lnexp'
RECIP = 'act'
A_RCP = 'dve'


class R:
    __slots__ = ('n', 'lw', 'rd')

    def __init__(self, n=''):
        self.n = n
        self.lw = None
        self.rd = []


class Op:
    __slots__ = ('eng', 'fn', 'deps', 'pos', 'chan', 'cidx', 'inc', 'cnt', 'rdeps')


class Sched:
    def __init__(self):
        self.ops = {e: [] for e in ENGS}
        self.chans = {}

    def add(self, eng, fn, rd=(), wr=(), chan=None):
        op = Op()
        op.eng = eng
        op.fn = fn
        op.chan = chan
        op.inc = False
        op.cnt = 0
        op.cidx = 0
        deps = set()
        for r in rd:
            if r.lw is not None:
                deps.add(r.lw)
        for r in wr:
            if r.lw is not None:
                deps.add(r.lw)
            deps.update(r.rd)
        op.deps = deps
        for r in rd:
            r.rd.append(op)
        for r in wr:
            r.lw = op
            r.rd = []
        op.pos = len(self.ops[eng])
        self.ops[eng].append(op)
        if chan is not None:
            c = self.chans.setdefault(chan, [0])
            c[0] += 1
            op.cidx = c[0]
        return op

    @staticmethod
    def reduce(deps, eng):
        best = {}
        for d in deps:
            if d.chan is not None:
                k = ('c', d.chan)
                if k not in best or d.cidx > best[k].cidx:
                    best[k] = d
            else:
                if d.eng == 'pe' and eng == 'pe':
                    continue
                k = d.eng
                if k not in best or d.pos > best[k].pos:
                    best[k] = d
        return list(best.values())

    def emit(self, nc, stack, final_chans):
        for e in ENGS:
            for op in self.ops[e]:
                op.rdeps = self.reduce(op.deps, e)
                op.deps = None
                for d in op.rdeps:
                    if d.chan is None:
                        d.inc = True
        sems = {}
        for e in ENGS:
            c = 0
            for op in self.ops[e]:
                if op.chan is None and op.inc:
                    c += 1
                    op.cnt = c
            sems[e] = stack.enter_context(nc.semaphore('s_' + e))
        csems = {}
        for ch in self.chans:
            csems[ch] = stack.enter_context(nc.semaphore('c_' + ch))
        ops = self.ops
        chans = self.chans

        def body_for(e):
            def body(eng):
                known = {}
                for op in ops[e]:
                    for d in op.rdeps:
                        if d.chan is not None:
                            k = ('c', d.chan)
                            v = 16 * d.cidx
                            sem = csems[d.chan]
                        else:
                            k = d.eng
                            v = d.cnt
                            sem = sems[d.eng]
                        if known.get(k, 0) >= v:
                            continue
                        known[k] = v
                        eng.wait_ge(sem, v)
                    ins = op.fn(eng)
                    if op.chan is not None:
                        ins.then_inc(csems[op.chan], 16)
                    elif op.inc:
                        ins.then_inc(sems[e], 1)
                if e == 'sp':
                    for ch in final_chans:
                        eng.wait_ge(csems[ch], 16 * chans[ch][0])
            return body

        with nc.Block() as block:
            block.sync(body_for('sp'))
            block.tensor(body_for('pe'))
            block.scalar(body_for('act'))
            block.vector(body_for('dve'))
            block.gpsimd(body_for('pool'))


def _pc(w, c):
    n = w.shape[1]
    return np.ascontiguousarray(w.reshape(c, 128, n).transpose(1, 0, 2)).reshape(128, c * n)


def build_weight_tiles(inp):
    tiles = []
    index = {}

    def put(name, arr):
        index[name] = len(tiles)
        tiles.append(arr)

    ffns = [(inp['ffn1_w_in'][0], inp['ffn1_w_out'][0]), (inp['ffn2_w_in'][0], inp['ffn2_w_out'][0]),
            (inp['ffn1_w_in'][1], inp['ffn1_w_out'][1]), (inp['ffn2_w_in'][1], inp['ffn2_w_out'][1])]
    for f, (w_in, w_out) in enumerate(ffns):
        for j in range(NJ):
            gu = np.concatenate([w_in[:, j * 128:(j + 1) * 128], w_in[:, DFF + j * 128:DFF + (j + 1) * 128]], axis=1)
            put(('f', f, 'in', j), _pc(gu, 8))
        for i in range(8):
            put(('f', f, 'out', i), _pc(w_out[:, i * 128:(i + 1) * 128], NJ))
    wqkv = inp['a_w_qkv'][0]
    for a in range(8):
        qk = np.concatenate([wqkv[:, a * 128:(a + 1) * 128], wqkv[:, 1024 + a * 128:1024 + (a + 1) * 128]], axis=1)
        put(('a', 'qk', a), _pc(qk, 8))
        put(('a', 'v', a), _pc(wqkv[:, 2048 + a * 128:2048 + (a + 1) * 128], 8))
    awo = inp['a_w_o'][0]
    for i in range(8):
        put(('a', 'o', i), _pc(awo[:, i * 128:(i + 1) * 128], 8))
    wd = inp['kv_w_down']
    put(('b', 'dckv'), _pc(wd[:, 0:256], 8))
    perm = np.concatenate([np.arange(32, 64), np.arange(0, 32)])
    rope = wd[:, 256:320]
    put(('b', 'drope'), _pc(np.concatenate([rope, rope[:, perm]], axis=1), 8))
    wdq = inp['b_w_dq'][0]
    for m in range(3):
        put(('b', 'dq', m), _pc(wdq[:, m * 256:(m + 1) * 256], 8))
    wup = inp['kv_w_up']
    wuq = inp['b_w_uq'][0]
    for h in range(8):
        wk = _pc(wup[:, h * 256:h * 256 + 128], 2)
        wv = _pc(wup[:, h * 256 + 128:h * 256 + 256], 2)
        wqn = _pc(wuq[:, h * 192:h * 192 + 128], 6)
        qr = wuq[:, h * 192 + 128:h * 192 + 192]
        wqr = _pc(qr, 6)
        wqs = _pc(qr[:, perm], 6)
        put(('b', 'head', h), np.concatenate([wk, wv, wqn, wqr, wqs], axis=1))
    bwo = inp['b_w_o'][0]
    for i in range(8):
        put(('b', 'o', i), _pc(bwo[:, i * 128:(i + 1) * 128], 8))
    blob = np.zeros((len(tiles), 128, WT), np.float32)
    widths = {}
    for name, k in index.items():
        a = tiles[k]
        blob[k, :, :a.shape[1]] = a
        widths[name] = a.shape[1]
    return blob, index, widths


def tile_index():
    index = {}
    widths = {}
    k = 0

    def put(name, w):
        nonlocal k
        index[name] = k
        widths[name] = w
        k += 1
    for f in range(4):
        for j in range(NJ):
            put(('f', f, 'in', j), 2048)
        for i in range(8):
            put(('f', f, 'out', i), 2816)
    for a in range(8):
        put(('a', 'qk', a), 2048)
        put(('a', 'v', a), 1024)
    for i in range(8):
        put(('a', 'o', i), 1024)
    put(('b', 'dckv'), 2048)
    put(('b', 'drope'), 1024)
    for m in range(3):
        put(('b', 'dq', m), 2048)
    for h in range(8):
        put(('b', 'head', h), 2048)
    for i in range(8):
        put(('b', 'o', i), 1024)
    return index, widths, k


def build_consts(inp):
    def g8(v):
        return v.reshape(-1, 128).T
    gains = np.zeros((128, NG), np.float32)
    gains[:, 0:8] = g8(inp['ffn1_norm'][0])
    gains[:, 8:16] = g8(inp['ffn1_norm'][1])
    gains[:, 16:24] = g8(inp['mix_norm'][0])
    gains[:, 24:32] = g8(inp['mix_norm'][1])
    gains[:, 32:40] = g8(inp['ffn2_norm'][0])
    gains[:, 40:48] = g8(inp['ffn2_norm'][1])
    gains[:, 48:56] = g8(inp['kv_norm'])
    gains[:, 56:64] = g8(inp['final_norm'])
    gains[:, 64:66] = g8(inp['kv_latent_norm'])
    gains[:, 66:72] = g8(inp['b_q_norm'][0])
    tab = inp['a_rel_bias'][0]
    cvec = np.ascontiguousarray(np.broadcast_to(tab[:, 256][None, :], (128, 16))).astype(np.float32)
    j = np.arange(128)[:, None, None]
    dl = np.arange(2)[None, :, None]
    i = np.arange(128)[None, None, :]
    idx = np.clip(128 * dl + i - j, -128, 128) + 128
    bias = tab[:, idx]
    bias = bias.reshape(8, 2, 128, 2, 128).transpose(0, 2, 1, 3, 4)
    bias = np.ascontiguousarray(bias).reshape(8, 128, 512).astype(np.float32)
    half = 32
    freqs = (10000.0 ** (-np.arange(half, dtype=np.float32) / half)).astype(np.float32)
    ang = np.arange(S, dtype=np.float32)[:, None] * freqs[None, :]
    cos = np.cos(ang).astype(np.float32).T
    sin = np.sin(ang).astype(np.float32).T
    cosT = np.ascontiguousarray(np.concatenate([cos, cos], axis=0))
    sinT = np.ascontiguousarray(np.concatenate([-sin, sin], axis=0))
    return gains, cvec, bias, cosT, sinT


def build_program(upto=99):
    nc = bass.Bass("TRN2", target_bir_lowering=False)
    windex, wwidth, ntiles = tile_index()
    xT = nc.dram_tensor("xT", [D, T], F32, kind="ExternalInput").ap()
    wblob = nc.dram_tensor("wblob", [ntiles, 128, WT], F32, kind="ExternalInput").ap()
    gains_d = nc.dram_tensor("gains", [128, NG], F32, kind="ExternalInput").ap()
    cvec_d = nc.dram_tensor("cvec", [128, 16], F32, kind="ExternalInput").ap()
    bias_d = nc.dram_tensor("biasA", [8, 128, 512], F32, kind="ExternalInput").ap()
    cos_d = nc.dram_tensor("cosT", [64, S], F32, kind="ExternalInput").ap()
    sin_d = nc.dram_tensor("sinT", [64, S], F32, kind="ExternalInput").ap()
    ident_d = nc.dram_tensor("ident", [128, 128], F32, kind="ExternalInput").ap()
    outT = nc.dram_tensor("outT", [D, T], F32, kind="ExternalOutput").ap()
    xs_d = [nc.dram_tensor("xs%d" % i, [D, T], F32, kind="Internal").ap() for i in range(6)]

    sch = Sched()
    stack = ExitStack()
    with stack:
        def sb_alloc(name, shape, dt):
            return stack.enter_context(nc.sbuf_tensor(name, shape, dt))

        def ps_alloc(name):
            return stack.enter_context(nc.psum_tensor(name, [128, 512], F32))

        xn = [(sb_alloc("xn%d" % i, [128, 8 * 512], F32), R()) for i in range(2)]
        rr = [(sb_alloc("rr%d" % i, [128, 512], F32), R()) for i in range(3)]
        stg = [(sb_alloc("stg%d" % i, [128, WT], F32), R()) for i in range(3)] if CAST != 'dma' else None
        wbf = [(sb_alloc("wbf%d" % i, [128, WT], BF16), R()) for i in range(4)]
        rstd = [(sb_alloc("rstd%d" % i, [128, 512], F32), R()) for i in range(2)]
        sg = [(sb_alloc("sg%d" % i, [128, 512], F32), R()) for i in range(2)]
        gains = sb_alloc("gains_sb", [128, NG], F32)
        gR = R()
        cvec = sb_alloc("cvec_sb", [128, 16], F32)
        cvR = R()
        ones = sb_alloc("ones_sb", [128, 128], BF16)
        onesR = R()
        A16N = 66560 if CAST == 'dma' else 48128
        arena16 = sb_alloc("arena16", [128, A16N], BF16)
        arena32 = sb_alloc("arena32", [128, 2048], F32)
        P = [(ps_alloc("ps%d" % i), R()) for i in range(8)]

        sch.add('sp', lambda e: e.dma_start(out=gains[:], in_=gains_d), wr=[gR], chan='gains')
        sch.add('sp', lambda e: e.dma_start(out=cvec[:], in_=cvec_d), wr=[cvR], chan='cvec')
        sch.add('pool', lambda e: e.memset(ones[:], 1.0), wr=[onesR])
        ones32 = sb_alloc("ones32_sb", [128, 128], F32)
        ones32R = R()
        sch.add('pool', lambda e: e.memset(ones32[:], 1.0), wr=[ones32R])
        ident = sb_alloc("ident_sb", [128, 128], BF16)
        identR = R()
        sch.add('pool', lambda e: e.dma_start(out=ident[:], in_=ident_d), wr=[identR], chan='ident')
        epsc = sb_alloc("eps_sb", [128, 8], F32)
        epsR = R()
        sch.add('pool', lambda e: e.memset(epsc[:], EPS), wr=[epsR])

        cnt = {}

        def nxt(name, n):
            v = cnt.get(name, 0)
            cnt[name] = v + 1
            return v % n

        arena_rs = []
        fence_ops = []

        def new_stage():
            ops = set()
            for r in arena_rs:
                if r.lw is not None:
                    ops.add(r.lw)
                ops.update(r.rd)
            del arena_rs[:]
            best = {}
            for d in ops:
                k = ('c', d.chan) if d.chan is not None else d.eng
                key = d.cidx if d.chan is not None else d.pos
                if k not in best or key > best[k][0]:
                    best[k] = (key, d)
            for d in fence_ops:
                k = ('c', d.chan) if d.chan is not None else d.eng
                key = d.cidx if d.chan is not None else d.pos
                if k not in best or key > best[k][0]:
                    best[k] = (key, d)
            del fence_ops[:]
            fence_ops.extend(v[1] for v in best.values())
            cur16[0] = 0
            cur32[0] = 0

        cur16 = [0]
        cur32 = [0]

        def AR(name=''):
            r = R(name)
            r.rd = list(fence_ops)
            arena_rs.append(r)
            return r

        def a16(n):
            o = cur16[0]
            cur16[0] += n
            assert cur16[0] <= A16N, cur16[0]
            return arena16[:, o:o + n]

        def a32(n):
            o = cur32[0]
            cur32[0] += n
            assert cur32[0] <= 2048
            return arena32[:, o:o + n]

        def MM(out, lhsT, rhs, start, stop, rd, wr):
            sch.add('pe', lambda e: e.matmul(out, lhsT, rhs, start=start, stop=stop), rd, wr)

        def RCP(out, in_, rd, wr, mode=None):
            if (mode or RECIP) == 'act':
                sch.add('act', lambda e: e.activation(out, in_, AF.Ln), rd, wr)
                sch.add('act', lambda e: e.activation(out, out, AF.Exp, scale=-1.0), wr, wr)
            else:
                sch.add('dve', lambda e: e.reciprocal(out, in_), rd, wr)

        def dma(out, in_, rd, wr, chan):
            return sch.add('sp', lambda e: e.dma_start(out=out, in_=in_), rd, wr, chan)

        steps = []

        def step(tiles, fn):
            steps.append((tiles, fn))

        def issue_tile(name, k):
            w = wwidth[name]
            si = k % 3
            bi = k % 4
            wb, wbR = wbf[bi]
            src = wblob[windex[name], :, 0:w]
            if CAST != 'dma':
                st, stR = stg[si]
            if CAST == 'dma':
                sch.add('pool', lambda e: e.dma_start(out=wb[:, 0:w], in_=src), [], [wbR], 'wq%d' % bi)
                return (wb, wbR)
            dma(st[:, 0:w], src, [], [stR], 'stg%d' % si)
            ce = CAST if isinstance(CAST, str) else CAST[k % len(CAST)]
            if ce == 'act':
                sch.add('act', lambda e: e.activation(wb[:, 0:w], st[:, 0:w], AF.Copy), [stR], [wbR])
            else:
                sch.add(ce, lambda e: e.tensor_copy(wb[:, 0:w], st[:, 0:w]), [stR], [wbR])
            return (wb, wbR)

        def dram_rs():
            return [[R() for _ in range(8)] for _ in range(NSB)]

        def xview(x, i, sb):
            return x[i * 128:(i + 1) * 128, sb * 512:(sb + 1) * 512]

        def norm_core(x3, xR, C, gcol, out3, outR, n, sq):
            if isinstance(sq, list):
                sq3, sqR = sq[nxt('sq', len(sq))]
            else:
                sq3, sqR = sq
            if SQ_ENG == 'act':
                sch.add('act', lambda e: e.activation(sq3[:, 0:C, :], x3, AF.Square), [xR], [sqR])
            else:
                sch.add(SQ_ENG, lambda e: e.tensor_tensor(sq3[:, 0:C, :], x3, x3, ALU.mult), [xR], [sqR])
            ps, psR = P[7]
            for c in range(C):
                MM(ps[:], ones[:], sq3[:, c, :], c == 0, c == C - 1, [sqR, onesR], [psR])
            k = nxt('rstd', 2)
            rs, rsR = rstd[k]
            if RSTD == 'lnexp':
                sch.add('act', lambda e: e.activation(rs[:], ps[:], AF.Ln, bias=epsc[:, 0:1], scale=1.0 / n),
                        [psR, epsR], [rsR])
                sch.add('act', lambda e: e.activation(rs[:], rs[:], AF.Exp, scale=-0.5), [rsR], [rsR])
            else:
                sch.add('act', lambda e: e.activation(rs[:], ps[:], AF.Sqrt, bias=epsc[:, 0:1], scale=1.0 / n),
                        [psR, epsR], [rsR])
                sch.add('dve', lambda e: e.reciprocal(rs[:], rs[:]), [rsR], [rsR])
            for c in range(C):
                sch.add('dve', (lambda c: lambda e: e.scalar_tensor_tensor(
                    out3[:, c, :], x3[:, c, :], gains[:, gcol + c:gcol + c + 1], rs[:], ALU.mult, ALU.mult))(c),
                    [xR, rsR, gR], [outR])

        def load_norm(src, srcR, sb, gcol, out3, outR, sq):
            k = nxt('xn', 2)
            xs, xR = xn[k]
            x3 = xs[:].rearrange('p (c t) -> p c t', c=8)
            dma(x3, src.rearrange('(c p) t -> p c t', p=128)[:, :, sb * 512:(sb + 1) * 512],
                list(srcR[sb]), [xR], 'xn%d' % k)
            norm_core(x3, xR, 8, gcol, out3, outR, float(D), sq)

        def resid(po, poR, src, srcR, dst, dstR, i, sb, scale):
            k = nxt('rr', 3)
            r, rR = rr[k]
            dma(r[:], xview(src, i, sb), [srcR[sb][i]], [rR], 'rl%d' % k)
            sch.add('dve', lambda e: e.scalar_tensor_tensor(r[:], po, scale, r[:], ALU.mult, ALU.add),
                    [poR, rR], [rR])
            dma(xview(dst, i, sb), r[:], [rR], [dstR[sb][i]], 'rs%d' % k)

        def ffn_stage(f, src, srcR, dst, dstR, gcol, final):
            st = {}
            NSF = 4 if CAST == 'dma' else 2
            TB = NSF * 512

            def setup(_w):
                new_stage()
                st['sq'] = (a16(4096).rearrange('p (c t) -> p c t', c=8), AR())
                st['hn'] = a16(8 * TB).rearrange('p (c t) -> p c t', c=8)
                st['hnR'] = [AR() for _ in range(NSF)]
                st['act'] = a16(NJ * TB).rearrange('p (j t) -> p j t', j=NJ)
                st['actR'] = [AR() for _ in range(NSF)]
            step([], setup)

            def norm_fn(b):
                def fn(_w):
                    for s in range(NSF):
                        load_norm(src, srcR, b * NSF + s, gcol, st['hn'][:, :, s * 512:(s + 1) * 512],
                                  st['hnR'][s], st['sq'])
                return fn

            def up_fn(b, j):
                def fn(w):
                    wb, wR = w[0]
                    w3 = wb[:, 0:2048].rearrange('p (c n) -> p c n', c=8)
                    hn = st['hn']
                    for s in range(NSF):
                        k = nxt('gu', 2)
                        pg, pgR = P[2 * k]
                        pu, puR = P[2 * k + 1]
                        for c in range(8):
                            MM(pg[:], w3[:, c, 0:128], hn[:, c, s * 512:(s + 1) * 512], c == 0, c == 7,
                               [wR, st['hnR'][s]], [pgR])
                        for c in range(8):
                            MM(pu[:], w3[:, c, 128:256], hn[:, c, s * 512:(s + 1) * 512], c == 0, c == 7,
                               [wR, st['hnR'][s]], [puR])
                        sgt, sgR = sg[k]
                        sch.add('act', lambda e, sgt=sgt, pg=pg: e.activation(sgt[:], pg[:], AF.Silu), [pgR], [sgR])
                        av = st['act'][:, j, s * 512:(s + 1) * 512]
                        sch.add('dve', lambda e, av=av, sgt=sgt, pu=pu: e.tensor_tensor(av, sgt[:], pu[:], ALU.mult),
                                [sgR, puR], [st['actR'][s]])
                return fn

            def down_fn(b, i):
                def fn(w):
                    wb, wR = w[0]
                    w3 = wb[:, 0:2816].rearrange('p (j n) -> p j n', j=NJ)
                    for s in range(NSF):
                        sb = b * NSF + s
                        k = nxt('ob', 3)
                        po, poR = P[4 + k]
                        for j in range(NJ):
                            MM(po[:], w3[:, j, :], st['act'][:, j, s * 512:(s + 1) * 512], j == 0, j == NJ - 1,
                               [wR, st['actR'][s]], [poR])
                        resid(po[:], poR, src, srcR, dst, dstR, i, sb, 0.5)
                return fn

            def final_fn(b):
                def fn(_w):
                    hold = {}

                    def ld(s):
                        sb = b * NSF + s
                        k = nxt('xn', 2)
                        xs, xR = xn[k]
                        x3 = xs[:].rearrange('p (c t) -> p c t', c=8)
                        dma(x3, dst.rearrange('(c p) t -> p c t', p=128)[:, :, sb * 512:(sb + 1) * 512],
                            list(dstR[sb]), [xR], 'xn%d' % k)
                        hold[s] = (k, x3, xR)

                    def fin(s):
                        sb = b * NSF + s
                        k, x3, xR = hold[s]
                        norm_core(x3, xR, 8, G_FINAL, x3, xR, float(D), st['sq'])
                        dma(outT.rearrange('(c p) t -> p c t', p=128)[:, :, sb * 512:(sb + 1) * 512], x3,
                            [xR], list(outR[sb]), 'fo%d' % k)
                    ld(0)
                    ld(1)
                    for s in range(NSF):
                        fin(s)
                        if s + 2 < NSF:
                            ld(s + 2)
                return fn

            NB = T // TB
            slots = {}

            def ld_fn(b, ss):
                def fn(_w):
                    for s_ in ss:
                        sb = b * NSF + s_
                        k = nxt('xn', 2)
                        xs, xR = xn[k]
                        x3 = xs[:].rearrange('p (c t) -> p c t', c=8)
                        dma(x3, src.rearrange('(c p) t -> p c t', p=128)[:, :, sb * 512:(sb + 1) * 512],
                            list(srcR[sb]), [xR], 'xn%d' % k)
                        slots[(b, s_)] = k
                return fn

            def nm_fn(b, s_):
                def fn(_w):
                    xs, xR = xn[slots[(b, s_)]]
                    x3 = xs[:].rearrange('p (c t) -> p c t', c=8)
                    norm_core(x3, xR, 8, gcol, st['hn'][:, :, s_ * 512:(s_ + 1) * 512], st['hnR'][s_],
                              float(D), st['sq'])
                return fn

            if NSF != 4:
                step([], norm_fn(0))
            else:
                step([], ld_fn(0, [0, 1]))
                step([], nm_fn(0, 0))
                step([], ld_fn(0, [2]))
                step([], nm_fn(0, 1))
                step([], ld_fn(0, [3]))
                step([], nm_fn(0, 2))
                step([], nm_fn(0, 3))
            for b in range(NB):
                for j in range(NJ):
                    step([('f', f, 'in', j)], up_fn(b, j))
                    if final and b > 0 and j == 5:
                        step([], final_fn(b - 1))
                for i in range(8):
                    nb_ = b + 1 < NB
                    if NSF == 4 and nb_ and i == 0:
                        step([], ld_fn(b + 1, [0, 1]))
                    step([('f', f, 'out', i)], down_fn(b, i))
                    if NSF != 4:
                        if i == 3 and nb_:
                            step([], norm_fn(b + 1))
                    elif nb_:
                        if i == 2:
                            step([], nm_fn(b + 1, 0))
                            step([], ld_fn(b + 1, [2]))
                        elif i == 3:
                            step([], nm_fn(b + 1, 1))
                            step([], ld_fn(b + 1, [3]))
                        elif i == 4:
                            step([], nm_fn(b + 1, 2))
                        elif i == 5:
                            step([], nm_fn(b + 1, 3))
            if final:
                step([], final_fn(NB - 1))

        def wo_load(st, i, src, srcR, t0):
            k = nxt('xn', 2)
            xs, xR = xn[k]
            dma(xs[:, 0:2048], src[i * 128:(i + 1) * 128, t0 * 512:(t0 + 4) * 512],
                [srcR[t0 + sbl][i] for sbl in range(4)], [xR], 'xn%d' % k)
            st['wo_slot'][i] = k

        def wo_fn(st, i, src, srcR, dst, dstR, t0):
            def fn(w):
                wb, wR = w[0]
                w3 = wb[:, 0:1024].rearrange('p (a n) -> p a n', a=8)
                ao = st['ao']
                if i == 0:
                    st['wo_slot'] = {}
                    wo_load(st, 0, src, srcR, t0)
                if i + 1 < 8:
                    wo_load(st, i + 1, src, srcR, t0)
                k = st['wo_slot'][i]
                xs, xR = xn[k]
                for sbl in range(4):
                    kk = nxt('pj', 2)
                    po, poR = P[kk]
                    for a in range(8):
                        MM(po[:], w3[:, a, :], ao[:, a, sbl * 512:(sbl + 1) * 512], a == 0, a == 7,
                           [wR, st['aoR'][sbl]], [poR])
                    r = xs[:, sbl * 512:(sbl + 1) * 512]
                    sch.add('dve', lambda e, r=r, po=po: e.tensor_tensor(r, po[:], r, ALU.add), [poR, xR], [xR])
                dma(dst[i * 128:(i + 1) * 128, t0 * 512:(t0 + 4) * 512], xs[:, 0:2048],
                    [xR], [dstR[t0 + sbl][i] for sbl in range(4)], 'xs%d' % k)
            return fn

        def a_stage(seq, src, srcR, dst, dstR):
            st = {}
            t0 = seq * 4

            def setup(_w):
                new_stage()
                st['sq'] = [(a16(4096).rearrange('p (c t) -> p c t', c=8), AR()) for _ in range(2)]
                st['hn'] = a16(8 * S).rearrange('p (c t) -> p c t', c=8)
                st['hnR'] = [AR() for _ in range(4)]
                st['ao'] = a16(8 * S).rearrange('p (c t) -> p c t', c=8)
                st['aoR'] = [AR() for _ in range(4)]
                st['qkv'] = []
                for _ in range(2):
                    qb_ = (a16(2 * S).rearrange('p (h t) -> p h t', h=2), AR())
                    kb_ = (a16(S), AR())
                    vb_ = (a16(S).rearrange('p (t n) -> p t n', n=128), AR())
                    st['qkv'].append((qb_, kb_, vb_))
                    q2_, q2R_ = qb_
                    sch.add('pool', (lambda q2_: lambda e: e.memset(q2_[64:128, 0, :], 0.0))(q2_), [], [q2R_])
                    sch.add('pool', (lambda q2_: lambda e: e.memset(q2_[0:64, 1, :], 0.0))(q2_), [], [q2R_])
                st['pT'] = [(a16(512), AR()) for _ in range(3)]
                st['bias'] = [(a32(512), AR()) for _ in range(2)]
                st['tmp'] = [(a32(256), AR()) for _ in range(3)]
                st['btb'] = [(a16(512), AR()) for _ in range(2)]
                for sbl in range(4):
                    load_norm(src, srcR, t0 + sbl, G_MIX[0], st['hn'][:, :, sbl * 512:(sbl + 1) * 512],
                              st['hnR'][sbl], st['sq'])
            step([], setup)

            def proj_groups(a, w):
                wqk, wqkR = w[0]
                wv, wvR = w[1]
                wqk3 = wqk[:, 0:2048].rearrange('p (c n) -> p c n', c=8)
                wv3 = wv[:, 0:1024].rearrange('p (c n) -> p c n', c=8)
                hn = st['hn']
                (qT, qR), (kT, kR), (v3, vR) = st['qkv'][a % 2]
                pp, ppR = P[1]
                groups = []
                for sbl in range(4):
                    cs = slice(sbl * 512, (sbl + 1) * 512)
                    hR = st['hnR'][sbl]

                    def gq(cs=cs, hR=hR):
                        for c in range(8):
                            MM(pp[:], wqk3[:, c, 0:128], hn[:, c, cs], c == 0, c == 7, [wqkR, hR], [ppR])
                        sch.add('dve', lambda e: e.tensor_scalar(
                            qT[0:64, 0, cs], pp[0:64, :], 0.125, None, ALU.mult), [ppR], [qR])
                        sch.add('dve', lambda e: e.tensor_scalar(
                            qT[64:128, 1, cs], pp[64:128, :], 0.125, None, ALU.mult), [ppR], [qR])

                    def gk(cs=cs, hR=hR):
                        for c in range(8):
                            MM(pp[:], wqk3[:, c, 128:256], hn[:, c, cs], c == 0, c == 7, [wqkR, hR], [ppR])
                        sch.add('dve', lambda e: e.tensor_copy(kT[:, cs], pp[:]), [ppR], [kR])

                    def gv(sbl=sbl, hR=hR):
                        for tt in range(4):
                            tok = slice(sbl * 512 + tt * 128, sbl * 512 + (tt + 1) * 128)
                            for c in range(8):
                                MM(pp[:, tt * 128:(tt + 1) * 128], hn[:, c, tok], wv3[:, c, :], c == 0, c == 7,
                                   [wvR, hR], [ppR])
                        sch.add('dve', lambda e: e.tensor_copy(
                            v3[:, sbl * 4:(sbl + 1) * 4, :], pp[:].rearrange('p (t n) -> p t n', n=128)),
                            [ppR], [vR])
                    groups += [gq, gk, gv]
                return groups

            def prologue_fn(w):
                for g in proj_groups(0, w):
                    g()
            step([('a', 'qk', 0), ('a', 'v', 0)], prologue_fn)

            def pair_fn(a):
                def fn(w):
                    nxt_groups = proj_groups(a + 1, w) if a + 1 < 8 else []
                    (qT, qR), (kT, kR), (v3, vR) = st['qkv'][a % 2]
                    bk = nxt('bias', 2)
                    bt, btR = st['bias'][bk]
                    dma(bt, bias_d[a], [], [btR], 'bias%d' % bk)
                    btb, btbR = st['btb'][bk]
                    for hh_ in range(2):
                        sch.add('dve', (lambda hh_: lambda e: e.tensor_scalar(
                            btb[:, hh_ * 256:(hh_ + 1) * 256], bt[:, hh_ * 256:(hh_ + 1) * 256],
                            cvec[:, 2 * a + hh_:2 * a + hh_ + 1], None, ALU.subtract))(hh_), [btR, cvR], [btbR])
                    units = []
                    for QB in range(4):
                        for hh in range(2):
                            rs_ = [r for r in (3, 4, 0, 1, 2, 5, 6, 7) if 4 * QB - 4 + r >= 0]
                            for n_, r in enumerate(rs_):
                                units.append((QB, hh, r, n_ == 0, n_ == len(rs_) - 1))
                    SB3 = [P[3], P[4], P[5]]

                    def rng(r):
                        return (0, (r + 1) * 128) if r <= 3 else ((r - 4) * 128, 512)

                    def scores(u):
                        QB, hh, r, first, last = units[u]
                        kt = 4 * QB - 4 + r
                        c0, c1 = rng(r)
                        kb = u % 3
                        sp_, spR = SB3[kb]
                        pT, pTR = st['pT'][kb]
                        tmp, tmpR = st['tmp'][kb]
                        h = 2 * a + hh
                        has_g = r >= 3
                        MM(sp_[:, c0:c1], kT[:, kt * 128:(kt + 1) * 128], qT[:, hh, QB * 512 + c0:QB * 512 + c1],
                           True, not has_g, [kR, qR], [spR])
                        if r >= 4:
                            g0, g1 = c0, min(c0 + 256, 512)
                            b0 = hh * 256
                            k0, k1 = g1, 512
                        elif r == 3:
                            g0, g1 = 0, 128
                            b0 = hh * 256 + 128
                            k0, k1 = 128, c1
                        else:
                            g0 = g1 = 0
                            b0 = 0
                            k0, k1 = 0, c1
                        if g1 > g0:
                            MM(sp_[:, g0:g1], ident[:], btb[:, b0:b0 + (g1 - g0)], False, True,
                               [identR, btbR], [spR])
                        sch.add('act', lambda e: e.activation(
                            pT[:, c0:c1], sp_[:, c0:c1], AF.Exp, bias=cvec[:, h:h + 1]), [spR, cvR], [pTR])
                        if r >= 4:
                            sch.add('pool', lambda e: e.memset(pT[64:128, c0:c0 + 64], 0.0), [], [pTR])
                        if r <= 3:
                            sch.add('pool', lambda e: e.memset(pT[0:64, c1 - 64:c1], 0.0), [], [pTR])

                    def pv_(u):
                        QB, hh, r, first, last = units[u]
                        kt = 4 * QB - 4 + r
                        c0, c1 = rng(r)
                        pl = slice(hh * 64, hh * 64 + 64)
                        kb = u % 3
                        pT, pTR = st['pT'][kb]
                        ob = (2 * QB + hh) % 2
                        po, poR = P[2] if ob == 0 else P[6]
                        pd, pdR = P[7] if ob == 0 else P[0]
                        MM(po[:, c0:c1], v3[:, kt, :], pT[:, c0:c1], first, last, [vR, pTR], [poR])
                        MM(pd[:, c0:c1], ones[:], pT[:, c0:c1], first, last, [onesR, pTR], [pdR])
                        if last:
                            rk = nxt('rstd', 2)
                            rd_, rdR = rstd[rk]
                            RCP(rd_[pl, :], pd[pl, :], [pdR], [rdR], 'dve' if hh == 0 else 'act')
                            ao = st['ao']
                            sch.add('dve', lambda e: e.tensor_tensor(
                                ao[pl, a, QB * 512:(QB + 1) * 512], po[pl, :], rd_[pl, :], ALU.mult),
                                [poR, rdR], [st['aoR'][QB]])

                    scores(0)
                    scores(1)
                    gi = 0
                    for u in range(len(units)):
                        if u + 2 < len(units):
                            scores(u + 2)
                        pv_(u)
                        want = (len(nxt_groups) * (u + 1)) // len(units)
                        while gi < want:
                            nxt_groups[gi]()
                            gi += 1
                return fn

            for a in range(8):
                step([('a', 'qk', a + 1), ('a', 'v', a + 1)] if a + 1 < 8 else [], pair_fn(a))
            for i in range(8):
                step([('a', 'o', i)], wo_fn(st, i, src, srcR, dst, dstR, t0))

        def b_stage(seq, kvsrc, kvsrcR, src, srcR, dst, dstR):
            st = {}
            t0 = seq * 4
            scale = float(192 ** -0.5)

            def load_tab(sbl):
                tk = nxt('tab', 2)
                tb, tbR = st['tab'][tk]
                dma(tb[0:64, 0:512], cos_d[:, sbl * 512:(sbl + 1) * 512], [], [tbR], 'tab%d' % tk)
                dma(tb[0:64, 512:1024], sin_d[:, sbl * 512:(sbl + 1) * 512], [], [tbR], 'tab%d' % tk)
                return tb, tbR

            def rope_out(pr, prR, psw, pswR, tb, tbR, out_ap, outR):
                if nxt('ropet', 2) == 0:
                    (t1, t1R), (t2, t2R) = sg[0], sg[1]
                else:
                    (t1, t1R), (t2, t2R) = rstd[0], rstd[1]
                sch.add('dve', lambda e: e.tensor_tensor(t1[0:64, :], pr[0:64, :], tb[0:64, 0:512], ALU.mult),
                        [prR, tbR], [t1R])
                sch.add('dve', lambda e: e.tensor_tensor(t2[0:64, :], psw[0:64, :], tb[0:64, 512:1024], ALU.mult),
                        [pswR, tbR], [t2R])
                sch.add('dve', lambda e: e.tensor_tensor(out_ap, t1[0:64, :], t2[0:64, :], ALU.add),
                        [t1R, t2R], [outR])

            def setup(_w):
                new_stage()
                st['sq'] = [(a16(4096).rearrange('p (c t) -> p c t', c=8), AR()) for _ in range(2)]
                st['ao'] = a16(8 * S).rearrange('p (c t) -> p c t', c=8)
                st['aoR'] = [AR() for _ in range(4)]
                st['cq'] = a16(6 * S).rearrange('p (c t) -> p c t', c=6)
                st['cqR'] = [AR() for _ in range(4)]
                st['ckv'] = a16(2 * S).rearrange('p (c t) -> p c t', c=2)
                st['ckvR'] = [AR() for _ in range(4)]
                st['kr'] = (a16(S), AR())
                st['kn'] = (a16(S), AR())
                st['vh'] = (a16(S).rearrange('p (t n) -> p t n', n=128), AR())
                st['qn'] = (a16(S), AR())
                st['qr'] = (a16(S), AR())
                st['pT'] = [(a16(512), AR()) for _ in range(3)]
                for nm in ('kr', 'qr'):
                    buf, bR = st[nm]
                    sch.add('pool', (lambda buf: lambda e: e.memset(buf[64:128, :], 0.0))(buf), [], [bR])
                st['tab'] = [(a32(1024), AR()) for _ in range(2)]
                for sbl in range(4):
                    load_norm(kvsrc, kvsrcR, t0 + sbl, G_KV, st['ao'][:, :, sbl * 512:(sbl + 1) * 512],
                              st['aoR'][sbl], st['sq'])
            step([], setup)

            def ckv_fn(w):
                wb, wR = w[0]
                w3 = wb[:, 0:2048].rearrange('p (c n) -> p c n', c=8)
                hn = st['ao']
                for sbl in range(4):
                    cs = slice(sbl * 512, (sbl + 1) * 512)
                    xk = nxt('xn', 2)
                    xs, xR = xn[xk]
                    x3 = xs[:].rearrange('p (c t) -> p c t', c=8)
                    for m in range(2):
                        k = nxt('pj', 2)
                        pp, ppR = P[k]
                        for c in range(8):
                            MM(pp[:], w3[:, c, m * 128:(m + 1) * 128], hn[:, c, cs], c == 0, c == 7,
                               [wR, st['aoR'][sbl]], [ppR])
                        sch.add('dve', lambda e, pp=pp, m=m, x3=x3: e.tensor_copy(x3[:, m, :], pp[:]), [ppR], [xR])
                    norm_core(x3[:, 0:2, :], xR, 2, G_LAT, st['ckv'][:, :, cs], st['ckvR'][sbl], 256.0, st['sq'])
            step([('b', 'dckv')], ckv_fn)

            def krope_fn(w):
                wb, wR = w[0]
                w3 = wb[:, 0:1024].rearrange('p (c n) -> p c n', c=8)
                hn = st['ao']
                kr, krR = st['kr']
                for sbl in range(4):
                    cs = slice(sbl * 512, (sbl + 1) * 512)
                    tb, tbR = load_tab(sbl)
                    pr, prR = P[2]
                    psw, pswR = P[3]
                    for c in range(8):
                        MM(pr[0:64, :], w3[:, c, 0:64], hn[:, c, cs], c == 0, c == 7, [wR, st['aoR'][sbl]], [prR])
                    for c in range(8):
                        MM(psw[0:64, :], w3[:, c, 64:128], hn[:, c, cs], c == 0, c == 7, [wR, st['aoR'][sbl]], [pswR])
                    rope_out(pr, prR, psw, pswR, tb, tbR, kr[0:64, cs], krR)
            step([('b', 'drope')], krope_fn)

            def qnorm_fn(_w):
                for sbl in range(4):
                    load_norm(src, srcR, t0 + sbl, G_MIX[1], st['ao'][:, :, sbl * 512:(sbl + 1) * 512],
                              st['aoR'][sbl], st['sq'])
            step([], qnorm_fn)

            def dq_fn(w):
                hn = st['ao']
                for sbl in range(4):
                    cs = slice(sbl * 512, (sbl + 1) * 512)
                    xk = nxt('xn', 2)
                    xs, xR = xn[xk]
                    x3 = xs[:].rearrange('p (c t) -> p c t', c=8)
                    for m in range(6):
                        wb, wR = w[m // 2]
                        w3 = wb[:, 0:2048].rearrange('p (c n) -> p c n', c=8)
                        k = nxt('pj', 2)
                        pp, ppR = P[k]
                        for c in range(8):
                            MM(pp[:], w3[:, c, (m % 2) * 128:(m % 2 + 1) * 128], hn[:, c, cs], c == 0, c == 7,
                               [wR, st['aoR'][sbl]], [ppR])
                        sch.add('dve', lambda e, pp=pp, m=m, x3=x3: e.tensor_copy(x3[:, m, :], pp[:]), [ppR], [xR])
                    norm_core(x3[:, 0:6, :], xR, 6, G_QN, st['cq'][:, :, cs], st['cqR'][sbl], 768.0, st['sq'])
            step([('b', 'dq', 0), ('b', 'dq', 1), ('b', 'dq', 2)], dq_fn)

            def head_fn(h):
                def fn(w):
                    wb, wR = w[0]
                    wk = wb[:, 0:256].rearrange('p (c n) -> p c n', c=2)
                    wv = wb[:, 256:512].rearrange('p (c n) -> p c n', c=2)
                    wqn = wb[:, 512:1280].rearrange('p (c n) -> p c n', c=6)
                    wqr = wb[:, 1280:1664].rearrange('p (c n) -> p c n', c=6)
                    wqs = wb[:, 1664:2048].rearrange('p (c n) -> p c n', c=6)
                    ckv = st['ckv']
                    cq = st['cq']
                    kn, knR = st['kn']
                    vh, vhR = st['vh']
                    qn, qnR = st['qn']
                    qr, qrR = st['qr']
                    kr, krR = st['kr']
                    for sbl in range(4):
                        cs = slice(sbl * 512, (sbl + 1) * 512)
                        cR = st['ckvR'][sbl]
                        qR_ = st['cqR'][sbl]
                        k = nxt('pj', 2)
                        pp, ppR = P[k]
                        for c in range(2):
                            MM(pp[:], wk[:, c, :], ckv[:, c, cs], c == 0, c == 1, [wR, cR], [ppR])
                        sch.add('act', lambda e, pp=pp, cs=cs: e.activation(kn[:, cs], pp[:], AF.Copy), [ppR], [knR])
                        k = nxt('pj', 2)
                        pp, ppR = P[k]
                        for tt in range(4):
                            tok = slice(sbl * 512 + tt * 128, sbl * 512 + (tt + 1) * 128)
                            for c in range(2):
                                MM(pp[:, tt * 128:(tt + 1) * 128], ckv[:, c, tok], wv[:, c, :], c == 0, c == 1,
                                   [wR, cR], [ppR])
                        sch.add('act', lambda e, pp=pp, sbl=sbl: e.activation(
                            vh[:, sbl * 4:(sbl + 1) * 4, :], pp[:].rearrange('p (t n) -> p t n', n=128), AF.Copy),
                            [ppR], [vhR])
                        k = nxt('pj', 2)
                        pp, ppR = P[k]
                        for c in range(6):
                            MM(pp[:], wqn[:, c, :], cq[:, c, cs], c == 0, c == 5, [wR, qR_], [ppR])
                        sch.add('act', lambda e, pp=pp, cs=cs: e.activation(qn[:, cs], pp[:], AF.Copy), [ppR], [qnR])
                        tb, tbR = load_tab(sbl)
                        pr, prR = P[2] if sbl % 2 == 0 else P[6]
                        psw, pswR = P[3] if sbl % 2 == 0 else P[7]
                        for c in range(6):
                            MM(pr[0:64, :], wqr[:, c, :], cq[:, c, cs], c == 0, c == 5, [wR, qR_], [prR])
                        for c in range(6):
                            MM(psw[0:64, :], wqs[:, c, :], cq[:, c, cs], c == 0, c == 5, [wR, qR_], [pswR])
                        rope_out(pr, prR, psw, pswR, tb, tbR, qr[0:64, cs], qrR)
                    units = [(Q, kt) for Q in range(4) for kt in range(4 * Q + 4)]

                    def scores(u):
                        Q, kt = units[u]
                        c0 = max(kt - 4 * Q, 0) * 128
                        kb = u % 3
                        sp_, spR = (P[4], P[5], P[0])[kb]
                        qs = slice(Q * 512 + c0, (Q + 1) * 512)
                        MM(sp_[:, c0:512], kn[:, kt * 128:(kt + 1) * 128], qn[:, qs], True, False,
                           [knR, qnR], [spR])
                        MM(sp_[:, c0:512], kr[:, kt * 128:(kt + 1) * 128], qr[:, qs], False, True,
                           [krR, qrR], [spR])
                        pT, pTR = st['pT'][kb]
                        sch.add('act', lambda e: e.activation(pT[:, c0:512], sp_[:, c0:512], AF.Exp, scale=scale),
                                [spR], [pTR])
                        if kt >= 4 * Q:
                            sch.add('pool', lambda e: e.memset(pT[64:128, c0:c0 + 64], 0.0), [], [pTR])

                    def pv_(u):
                        Q, kt = units[u]
                        c0 = max(kt - 4 * Q, 0) * 128
                        kb = u % 3
                        pT, pTR = st['pT'][kb]
                        nkt = 4 * Q + 4
                        po, poR = P[6] if Q % 2 == 0 else P[2]
                        pd, pdR = P[7] if Q % 2 == 0 else P[3]
                        MM(po[:, c0:512], vh[:, kt, :], pT[:, c0:512], kt == 0, kt == nkt - 1, [vhR, pTR], [poR])
                        MM(pd[:, c0:512], ones[:], pT[:, c0:512], kt == 0, kt == nkt - 1, [onesR, pTR], [pdR])
                        if kt == nkt - 1:
                            rk = nxt('rstd', 2)
                            rd_, rdR = rstd[rk]
                            RCP(rd_[:], pd[:], [pdR], [rdR])
                            ao = st['ao']
                            sch.add('dve', lambda e: e.tensor_tensor(
                                ao[:, h, Q * 512:(Q + 1) * 512], po[:], rd_[:], ALU.mult),
                                [poR, rdR], [st['aoR'][Q]])

                    scores(0)
                    scores(1)
                    for u in range(len(units)):
                        if u + 2 < len(units):
                            scores(u + 2)
                        pv_(u)
                return fn

            for h in range(8):
                step([('b', 'head', h)], head_fn(h))
            for i in range(8):
                step([('b', 'o', i)], wo_fn(st, i, src, srcR, dst, dstR, t0))

        xR_in = dram_rs()
        xsR = [dram_rs() for _ in range(6)]
        outR = dram_rs()
        stages = []
        stages.append(lambda dst, dR: ffn_stage(0, xT, xR_in, dst, dR, G_FFN1[0], False))
        stages.append(lambda dst, dR: [a_stage(q, xs_d[0], xsR[0], dst, dR) for q in range(NSEQ)])
        stages.append(lambda dst, dR: ffn_stage(1, xs_d[1], xsR[1], dst, dR, G_FFN2[0], False))
        stages.append(lambda dst, dR: ffn_stage(2, xs_d[2], xsR[2], dst, dR, G_FFN1[1], False))
        stages.append(lambda dst, dR: [b_stage(q, xs_d[2], xsR[2], xs_d[3], xsR[3], dst, dR) for q in range(NSEQ)])
        stages.append(lambda dst, dR: ffn_stage(3, xs_d[4], xsR[4], xs_d[5], xsR[5], G_FFN2[1], True))
        nst = min(upto, len(stages))
        for k in range(nst):
            if k == nst - 1:
                stages[k](outT, outR)
            else:
                stages[k](xs_d[k], xsR[k])

        tile_seq = []
        first_tile = []
        for tiles, fn in steps:
            first_tile.append(len(tile_seq))
            tile_seq.extend(tiles)
        issued = {}
        nissued = 0
        for m, (tiles, fn) in enumerate(steps):
            a = first_tile[m]
            lim = min(len(tile_seq), a + 4)
            while nissued < lim:
                issued[nissued] = issue_tile(tile_seq[nissued], nissued)
                nissued += 1
            fn([issued[a + k] for k in range(len(tiles))])

        final_chans = [ch for ch in sch.chans if ch.startswith('rs') or ch.startswith('fo') or ch.startswith('xs')]
        sch.emit(nc, stack, final_chans)
        stats = {e: len(sch.ops[e]) for e in ENGS}
        stats['maxcnt'] = {e: max([op.cnt for op in sch.ops[e]] + [0]) for e in ENGS}
        stats['maxchan'] = max(v[0] for v in sch.chans.values())
        stats['nsem'] = len(sch.chans) + 5
    return nc, stats


def make_in_maps(inputs):
    inp = {k: np.asarray(v, dtype=np.float32) for k, v in inputs.items()}
    blob, index, widths = build_weight_tiles(inp)
    i2, w2, nt = tile_index()
    assert i2 == index and w2 == widths and nt == blob.shape[0]
    gains, cvec, bias, cosT, sinT = build_consts(inp)
    x = inp['x']
    in_maps = []
    for c in range(NCORES):
        xc = np.ascontiguousarray(x[c * NSEQ:(c + 1) * NSEQ].reshape(T, D).T)
        in_maps.append({"xT": xc, "wblob": blob, "gains": gains, "cvec": cvec, "biasA": bias,
                        "cosT": cosT, "sinT": sinT, "ident": np.eye(128, dtype=np.float32)})
    return in_maps


def kernel(**inputs):
    in_maps = make_in_maps(inputs)
    nc, _ = build_program()
    res = run_bass_kernel_spmd(nc, in_maps, core_ids=list(range(NCORES)))
    outs = []
    for c in range(NCORES):
        o = np.asarray(res.results[c]["outT"], dtype=np.float32)
        outs.append(o.T.reshape(NSEQ, S, D))
    return np.concatenate(outs, axis=0).astype(np.float32)
```

```python
import numpy as np
from contextlib import ExitStack
import concourse.bass as bass
import concourse.mybir as mybir
from concourse.bass_utils import run_bass_kernel_spmd

F32 = mybir.dt.float32
BF16 = mybir.dt.bfloat16
AF = mybir.ActivationFunctionType
ALU = mybir.AluOpType

NCORES = 8
D = 1024
S = 2048
NSEQ = 2
T = NSEQ * S
NSB = T // 512
DFF = 2816
NJ = 22
WT = 2816
EPS = 1e-6
ENGS = ['sp', 'pe', 'act', 'dve', 'pool']

G_FFN1 = [0, 8]
G_MIX = [16, 24]
G_FFN2 = [32, 40]
G_KV = 48
G_FINAL = 56
G_LAT = 64
G_QN = 66
NG = 80
CAST = 'dma'
SQ_ENG = 'act'
RSTD = 'lnexp'
RECIP = 'act'
A_RCP = 'dve'


class R:
    __slots__ = ('n', 'lw', 'rd')

    def __init__(self, n=''):
        self.n = n
        self.lw = None
        self.rd = []


class Op:
    __slots__ = ('eng', 'fn', 'deps', 'pos', 'chan', 'cidx', 'inc', 'cnt', 'rdeps')


class Sched:
    def __init__(self):
        self.ops = {e: [] for e in ENGS}
        self.chans = {}

    def add(self, eng, fn, rd=(), wr=(), chan=None):
        op = Op()
        op.eng = eng
        op.fn = fn
        op.chan = chan
        op.inc = False
        op.cnt = 0
        op.cidx = 0
        deps = set()
        for r in rd:
            if r.lw is not None:
                deps.add(r.lw)
        for r in wr:
            if r.lw is not None:
                deps.add(r.lw)
            deps.update(r.rd)
        op.deps = deps
        for r in rd:
            r.rd.append(op)
        for r in wr:
            r.lw = op
            r.rd = []
        op.pos = len(self.ops[eng])
        self.ops[eng].append(op)
        if chan is not None:
            c = self.chans.setdefault(chan, [0])
            c[0] += 1
            op.cidx = c[0]
        return op

    @staticmethod
    def reduce(deps, eng):
        best = {}
        for d in deps:
            if d.chan is not None:
                k = ('c', d.chan)
                if k not in best or d.cidx > best[k].cidx:
                    best[k] = d
            else:
                if d.eng == 'pe' and eng == 'pe':
                    continue
                k = d.eng
                if k not in best or d.pos > best[k].pos:
                    best[k] = d
        return list(best.values())

    def emit(self, nc, stack, final_chans):
        for e in ENGS:
            for op in self.ops[e]:
                op.rdeps = self.reduce(op.deps, e)
                op.deps = None
                for d in op.rdeps:
                    if d.chan is None:
                        d.inc = True
        sems = {}
        for e in ENGS:
            c = 0
            for op in self.ops[e]:
                if op.chan is None and op.inc:
                    c += 1
                    op.cnt = c
            sems[e] = stack.enter_context(nc.semaphore('s_' + e))
        csems = {}
        for ch in self.chans:
            csems[ch] = stack.enter_context(nc.semaphore('c_' + ch))
        ops = self.ops
        chans = self.chans

        def body_for(e):
            def body(eng):
                known = {}
                for op in ops[e]:
                    for d in op.rdeps:
                        if d.chan is not None:
                            k = ('c', d.chan)
                            v = 16 * d.cidx
                            sem = csems[d.chan]
                        else:
                            k = d.eng
                            v = d.cnt
                            sem = sems[d.eng]
                        if known.get(k, 0) >= v:
                            continue
                        known[k] = v
                        eng.wait_ge(sem, v)
                    ins = op.fn(eng)
                    if op.chan is not None:
                        ins.then_inc(csems[op.chan], 16)
                    elif op.inc:
                        ins.then_inc(sems[e], 1)
                if e == 'sp':
                    for ch in final_chans:
                        eng.wait_ge(csems[ch], 16 * chans[ch][0])
            return body

        with nc.Block() as block:
            block.sync(body_for('sp'))
            block.tensor(body_for('pe'))
            block.scalar(body_for('act'))
            block.vector(body_for('dve'))
            block.gpsimd(body_for('pool'))


def _pc(w, c):
    n = w.shape[1]
    return np.ascontiguousarray(w.reshape(c, 128, n).transpose(1, 0, 2)).reshape(128, c * n)


def build_weight_tiles(inp):
    tiles = []
    index = {}

    def put(name, arr):
        index[name] = len(tiles)
        tiles.append(arr)

    ffns = [(inp['ffn1_w_in'][0], inp['ffn1_w_out'][0]), (inp['ffn2_w_in'][0], inp['ffn2_w_out'][0]),
            (inp['ffn1_w_in'][1], inp['ffn1_w_out'][1]), (inp['ffn2_w_in'][1], inp['ffn2_w_out'][1])]
    for f, (w_in, w_out) in enumerate(ffns):
        for j in range(NJ):
            gu = np.concatenate([w_in[:, j * 128:(j + 1) * 128], w_in[:, DFF + j * 128:DFF + (j + 1) * 128]], axis=1)
            put(('f', f, 'in', j), _pc(gu, 8))
        for i in range(8):
            put(('f', f, 'out', i), _pc(w_out[:, i * 128:(i + 1) * 128], NJ))
    wqkv = inp['a_w_qkv'][0]
    for a in range(8):
        qk = np.concatenate([wqkv[:, a * 128:(a + 1) * 128], wqkv[:, 1024 + a * 128:1024 + (a + 1) * 128]], axis=1)
        put(('a', 'qk', a), _pc(qk, 8))
        put(('a', 'v', a), _pc(wqkv[:, 2048 + a * 128:2048 + (a + 1) * 128], 8))
    awo = inp['a_w_o'][0]
    for i in range(8):
        put(('a', 'o', i), _pc(awo[:, i * 128:(i + 1) * 128], 8))
    wd = inp['kv_w_down']
    put(('b', 'dckv'), _pc(wd[:, 0:256], 8))
    perm = np.concatenate([np.arange(32, 64), np.arange(0, 32)])
    rope = wd[:, 256:320]
    put(('b', 'drope'), _pc(np.concatenate([rope, rope[:, perm]], axis=1), 8))
    wdq = inp['b_w_dq'][0]
    for m in range(3):
        put(('b', 'dq', m), _pc(wdq[:, m * 256:(m + 1) * 256], 8))
    wup = inp['kv_w_up']
    wuq = inp['b_w_uq'][0]
    for h in range(8):
        wk = _pc(wup[:, h * 256:h * 256 + 128], 2)
        wv = _pc(wup[:, h * 256 + 128:h * 256 + 256], 2)
        wqn = _pc(wuq[:, h * 192:h * 192 + 128], 6)
        qr = wuq[:, h * 192 + 128:h * 192 + 192]
        wqr = _pc(qr, 6)
        wqs = _pc(qr[:, perm], 6)
        put(('b', 'head', h), np.concatenate([wk, wv, wqn, wqr, wqs], axis=1))
    bwo = inp['b_w_o'][0]
    for i in range(8):
        put(('b', 'o', i), _pc(bwo[:, i * 128:(i + 1) * 128], 8))
    blob = np.zeros((len(tiles), 128, WT), np.float32)
    widths = {}
    for name, k in index.items():
        a = tiles[k]
        blob[k, :, :a.shape[1]] = a
        widths[name] = a.shape[1]
    return blob, index, widths


def tile_index():
    index = {}
    widths = {}
    k = 0

    def put(name, w):
        nonlocal k
        index[name] = k
        widths[name] = w
        k += 1
    for f in range(4):
        for j in range(NJ):
            put(('f', f, 'in', j), 2048)
        for i in range(8):
            put(('f', f, 'out', i), 2816)
    for a in range(8):
        put(('a', 'qk', a), 2048)
        put(('a', 'v', a), 1024)
    for i in range(8):
        put(('a', 'o', i), 1024)
    put(('b', 'dckv'), 2048)
    put(('b', 'drope'), 1024)
    for m in range(3):
        put(('b', 'dq', m), 2048)
    for h in range(8):
        put(('b', 'head', h), 2048)
    for i in range(8):
        put(('b', 'o', i), 1024)
    return index, widths, k


def build_consts(inp):
    def g8(v):
        return v.reshape(-1, 128).T
    gains = np.zeros((128, NG), np.float32)
    gains[:, 0:8] = g8(inp['ffn1_norm'][0])
    gains[:, 8:16] = g8(inp['ffn1_norm'][1])
    gains[:, 16:24] = g8(inp['mix_norm'][0])
    gains[:, 24:32] = g8(inp['mix_norm'][1])
    gains[:, 32:40] = g8(inp['ffn2_norm'][0])
    gains[:, 40:48] = g8(inp['ffn2_norm'][1])
    gains[:, 48:56] = g8(inp['kv_norm'])
    gains[:, 56:64] = g8(inp['final_norm'])
    gains[:, 64:66] = g8(inp['kv_latent_norm'])
    gains[:, 66:72] = g8(inp['b_q_norm'][0])
    tab = inp['a_rel_bias'][0]
    cvec = np.ascontiguousarray(np.broadcast_to(tab[:, 256][None, :], (128, 16))).astype(np.float32)
    j = np.arange(128)[:, None, None]
    dl = np.arange(2)[None, :, None]
    i = np.arange(128)[None, None, :]
    idx = np.clip(128 * dl + i - j, -128, 128) + 128
    bias = tab[:, idx]
    bias = bias.reshape(8, 2, 128, 2, 128).transpose(0, 2, 1, 3, 4)
    bias = np.ascontiguousarray(bias).reshape(8, 128, 512).astype(np.float32)
    half = 32
    freqs = (10000.0 ** (-np.arange(half, dtype=np.float32) / half)).astype(np.float32)
    ang = np.arange(S, dtype=np.float32)[:, None] * freqs[None, :]
    cos = np.cos(ang).astype(np.float32).T
    sin = np.sin(ang).astype(np.float32).T
    cosT = np.ascontiguousarray(np.concatenate([cos, cos], axis=0))
    sinT = np.ascontiguousarray(np.concatenate([-sin, sin], axis=0))
    return gains, cvec, bias, cosT, sinT


def build_program(upto=99):
    nc = bass.Bass("TRN2", target_bir_lowering=False)
    windex, wwidth, ntiles = tile_index()
    xT = nc.dram_tensor("xT", [D, T], F32, kind="ExternalInput").ap()
    wblob = nc.dram_tensor("wblob", [ntiles, 128, WT], F32, kind="ExternalInput").ap()
    gains_d = nc.dram_tensor("gains", [128, NG], F32, kind="ExternalInput").ap()
    cvec_d = nc.dram_tensor("cvec", [128, 16], F32, kind="ExternalInput").ap()
    bias_d = nc.dram_tensor("biasA", [8, 128, 512], F32, kind="ExternalInput").ap()
    cos_d = nc.dram_tensor("cosT", [64, S], F32, kind="ExternalInput").ap()
    sin_d = nc.dram_tensor("sinT", [64, S], F32, kind="ExternalInput").ap()
    ident_d = nc.dram_tensor("ident", [128, 128], F32, kind="ExternalInput").ap()
    outT = nc.dram_tensor("outT", [D, T], F32, kind="ExternalOutput").ap()
    xs_d = [nc.dram_tensor("xs%d" % i, [D, T], F32, kind="Internal").ap() for i in range(6)]

    sch = Sched()
    stack = ExitStack()
    with stack:
        def sb_alloc(name, shape, dt):
            return stack.enter_context(nc.sbuf_tensor(name, shape, dt))

        def ps_alloc(name):
            return stack.enter_context(nc.psum_tensor(name, [128, 512], F32))

        xn = [(sb_alloc("xn%d" % i, [128, 8 * 512], F32), R()) for i in range(2)]
        rr = [(sb_alloc("rr%d" % i, [128, 512], F32), R()) for i in range(3)]
        stg = [(sb_alloc("stg%d" % i, [128, WT], F32), R()) for i in range(3)] if CAST != 'dma' else None
        wbf = [(sb_alloc("wbf%d" % i, [128, WT], BF16), R()) for i in range(4)]
        rstd = [(sb_alloc("rstd%d" % i, [128, 512], F32), R()) for i in range(2)]
        sg = [(sb_alloc("sg%d" % i, [128, 512], F32), R()) for i in range(2)]
        gains = sb_alloc("gains_sb", [128, NG], F32)
        gR = R()
        cvec = sb_alloc("cvec_sb", [128, 16], F32)
        cvR = R()
        ones = sb_alloc("ones_sb", [128, 128], BF16)
        onesR = R()
        A16N = 66560 if CAST == 'dma' else 48128
        arena16 = sb_alloc("arena16", [128, A16N], BF16)
        arena32 = sb_alloc("arena32", [128, 2048], F32)
        P = [(ps_alloc("ps%d" % i), R()) for i in range(8)]

        sch.add('sp', lambda e: e.dma_start(out=gains[:], in_=gains_d), wr=[gR], chan='gains')
        sch.add('sp', lambda e: e.dma_start(out=cvec[:], in_=cvec_d), wr=[cvR], chan='cvec')
        sch.add('pool', lambda e: e.memset(ones[:], 1.0), wr=[onesR])
        ones32 = sb_alloc("ones32_sb", [128, 128], F32)
        ones32R = R()
        sch.add('pool', lambda e: e.memset(ones32[:], 1.0), wr=[ones32R])
        ident = sb_alloc("ident_sb", [128, 128], BF16)
        identR = R()
        sch.add('pool', lambda e: e.dma_start(out=ident[:], in_=ident_d), wr=[identR], chan='ident')
        epsc = sb_alloc("eps_sb", [128, 8], F32)
        epsR = R()
        sch.add('pool', lambda e: e.memset(epsc[:], EPS), wr=[epsR])

        cnt = {}

        def nxt(name, n):
            v = cnt.get(name, 0)
            cnt[name] = v + 1
            return v % n

        arena_rs = []
        fence_ops = []

        def new_stage():
            ops = set()
            for r in arena_rs:
                if r.lw is not None:
                    ops.add(r.lw)
                ops.update(r.rd)
            del arena_rs[:]
            best = {}
            for d in ops:
                k = ('c', d.chan) if d.chan is not None else d.eng
                key = d.cidx if d.chan is not None else d.pos
                if k not in best or key > best[k][0]:
                    best[k] = (key, d)
            for d in fence_ops:
                k = ('c', d.chan) if d.chan is not None else d.eng
                key = d.cidx if d.chan is not None else d.pos
                if k not in best or key > best[k][0]:
                    best[k] = (key, d)
            del fence_ops[:]
            fence_ops.extend(v[1] for v in best.values())
            cur16[0] = 0
            cur32[0] = 0

        cur16 = [0]
        cur32 = [0]

        def AR(name=''):
            r = R(name)
            r.rd = list(fence_ops)
            arena_rs.append(r)
            return r

        def a16(n):
            o = cur16[0]
            cur16[0] += n
            assert cur16[0] <= A16N, cur16[0]
            return arena16[:, o:o + n]

        def a32(n):
            o = cur32[0]
            cur32[0] += n
            assert cur32[0] <= 2048
            return arena32[:, o:o + n]

        def MM(out, lhsT, rhs, start, stop, rd, wr):
            sch.add('pe', lambda e: e.matmul(out, lhsT, rhs, start=start, stop=stop), rd, wr)

        def RCP(out, in_, rd, wr, mode=None):
            if (mode or RECIP) == 'act':
                sch.add('act', lambda e: e.activation(out, in_, AF.Ln), rd, wr)
                sch.add('act', lambda e: e.activation(out, out, AF.Exp, scale=-1.0), wr, wr)
            else:
                sch.add('dve', lambda e: e.reciprocal(out, in_), rd, wr)

        def dma(out, in_, rd, wr, chan):
            return sch.add('sp', lambda e: e.dma_start(out=out, in_=in_), rd, wr, chan)

        steps = []

        def step(tiles, fn):
            steps.append((tiles, fn))

        def issue_tile(name, k):
            w = wwidth[name]
            si = k % 3
            bi = k % 4
            wb, wbR = wbf[bi]
            src = wblob[windex[name], :, 0:w]
            if CAST != 'dma':
                st, stR = stg[si]
            if CAST == 'dma':
                sch.add('pool', lambda e: e.dma_start(out=wb[:, 0:w], in_=src), [], [wbR], 'wq%d' % bi)
                return (wb, wbR)
            dma(st[:, 0:w], src, [], [stR], 'stg%d' % si)
            ce = CAST if isinstance(CAST, str) else CAST[k % len(CAST)]
            if ce == 'act':
                sch.add('act', lambda e: e.activation(wb[:, 0:w], st[:, 0:w], AF.Copy), [stR], [wbR])
            else:
                sch.add(ce, lambda e: e.tensor_copy(wb[:, 0:w], st[:, 0:w]), [stR], [wbR])
            return (wb, wbR)

        def dram_rs():
            return [[R() for _ in range(8)] for _ in range(NSB)]

        def xview(x, i, sb):
            return x[i * 128:(i + 1) * 128, sb * 512:(sb + 1) * 512]

        def norm_core(x3, xR, C, gcol, out3, outR, n, sq, defer=None):
            if isinstance(sq, list):
                sq3, sqR = sq[nxt('sq', len(sq))]
            else:
                sq3, sqR = sq
            if SQ_ENG == 'act':
                sch.add('act', lambda e: e.activation(sq3[:, 0:C, :], x3, AF.Square), [xR], [sqR])
            else:
                sch.add(SQ_ENG, lambda e: e.tensor_tensor(sq3[:, 0:C, :], x3, x3, ALU.mult), [xR], [sqR])
            ps, psR = P[7]
            for c in range(C):
                MM(ps[:], ones[:], sq3[:, c, :], c == 0, c == C - 1, [sqR, onesR], [psR])
            k = nxt('rstd', 2)
            rs, rsR = rstd[k]
            if RSTD == 'lnexp':
                sch.add('act', lambda e: e.activation(rs[:], ps[:], AF.Ln, bias=epsc[:, 0:1], scale=1.0 / n),
                        [psR, epsR], [rsR])
                sch.add('act', lambda e: e.activation(rs[:], rs[:], AF.Exp, scale=-0.5), [rsR], [rsR])
            else:
                sch.add('act', lambda e: e.activation(rs[:], ps[:], AF.Sqrt, bias=epsc[:, 0:1], scale=1.0 / n),
                        [psR, epsR], [rsR])
                sch.add('dve', lambda e: e.reciprocal(rs[:], rs[:]), [rsR], [rsR])
            for c in range(C):
                def emit_c(c=c):
                    sch.add('dve', lambda e: e.scalar_tensor_tensor(
                        out3[:, c, :], x3[:, c, :], gains[:, gcol + c:gcol + c + 1], rs[:], ALU.mult, ALU.mult),
                        [xR, rsR, gR], [outR])
                if defer is None:
                    emit_c()
                else:
                    defer.append(emit_c)

        def load_norm(src, srcR, sb, gcol, out3, outR, sq):
            k = nxt('xn', 2)
            xs, xR = xn[k]
            x3 = xs[:].rearrange('p (c t) -> p c t', c=8)
            dma(x3, src.rearrange('(c p) t -> p c t', p=128)[:, :, sb * 512:(sb + 1) * 512],
                list(srcR[sb]), [xR], 'xn%d' % k)
            norm_core(x3, xR, 8, gcol, out3, outR, float(D), sq)

        def resid(po, poR, src, srcR, dst, dstR, i, sb, scale):
            k = nxt('rr', 3)
            r, rR = rr[k]
            dma(r[:], xview(src, i, sb), [srcR[sb][i]], [rR], 'rl%d' % k)
            sch.add('dve', lambda e: e.scalar_tensor_tensor(r[:], po, scale, r[:], ALU.mult, ALU.add),
                    [poR, rR], [rR])
            dma(xview(dst, i, sb), r[:], [rR], [dstR[sb][i]], 'rs%d' % k)

        def ffn_stage(f, src, srcR, dst, dstR, gcol, final):
            st = {}
            NSF = 4 if CAST == 'dma' else 2
            TB = NSF * 512

            def setup(_w):
                new_stage()
                st['sq'] = (a16(4096).rearrange('p (c t) -> p c t', c=8), AR())
                st['hn'] = a16(8 * TB).rearrange('p (c t) -> p c t', c=8)
                st['hnR'] = [AR() for _ in range(NSF)]
                st['act'] = a16(NJ * TB).rearrange('p (j t) -> p j t', j=NJ)
                st['actR'] = [AR() for _ in range(NSF)]
            step([], setup)

            def norm_fn(b):
                def fn(_w):
                    for s in range(NSF):
                        load_norm(src, srcR, b * NSF + s, gcol, st['hn'][:, :, s * 512:(s + 1) * 512],
                                  st['hnR'][s], st['sq'])
                return fn

            def up_fn(b, j):
                def fn(w):
                    wb, wR = w[0]
                    w3 = wb[:, 0:2048].rearrange('p (c n) -> p c n', c=8)
                    hn = st['hn']
                    for s in range(NSF):
                        k = nxt('gu', 2)
                        pg, pgR = P[2 * k]
                        pu, puR = P[2 * k + 1]
                        for c in range(8):
                            MM(pg[:], w3[:, c, 0:128], hn[:, c, s * 512:(s + 1) * 512], c == 0, c == 7,
                               [wR, st['hnR'][s]], [pgR])
                        for c in range(8):
                            MM(pu[:], w3[:, c, 128:256], hn[:, c, s * 512:(s + 1) * 512], c == 0, c == 7,
                               [wR, st['hnR'][s]], [puR])
                        sgt, sgR = sg[k]
                        sch.add('act', lambda e, sgt=sgt, pg=pg: e.activation(sgt[:], pg[:], AF.Silu), [pgR], [sgR])
                        av = st['act'][:, j, s * 512:(s + 1) * 512]
                        sch.add('dve', lambda e, av=av, sgt=sgt, pu=pu: e.tensor_tensor(av, sgt[:], pu[:], ALU.mult),
                                [sgR, puR], [st['actR'][s]])
                return fn

            def down_fn(b, i):
                def fn(w):
                    wb, wR = w[0]
                    w3 = wb[:, 0:2816].rearrange('p (j n) -> p j n', j=NJ)
                    for s in range(NSF):
                        sb = b * NSF + s
                        k = nxt('ob', 3)
                        po, poR = P[4 + k]
                        for j in range(NJ):
                            MM(po[:], w3[:, j, :], st['act'][:, j, s * 512:(s + 1) * 512], j == 0, j == NJ - 1,
                               [wR, st['actR'][s]], [poR])
                        resid(po[:], poR, src, srcR, dst, dstR, i, sb, 0.5)
                        for _ in range(2):
                            if pend:
                                pend.pop(0)()
                    if i == 7:
                        while pend:
                            pend.pop(0)()
                return fn

            def final_fn(b):
                def fn(_w):
                    hold = {}

                    def ld(s):
                        sb = b * NSF + s
                        k = nxt('xn', 2)
                        xs, xR = xn[k]
                        x3 = xs[:].rearrange('p (c t) -> p c t', c=8)
                        dma(x3, dst.rearrange('(c p) t -> p c t', p=128)[:, :, sb * 512:(sb + 1) * 512],
                            list(dstR[sb]), [xR], 'xn%d' % k)
                        hold[s] = (k, x3, xR)

                    def fin(s):
                        sb = b * NSF + s
                        k, x3, xR = hold[s]
                        norm_core(x3, xR, 8, G_FINAL, x3, xR, float(D), st['sq'])
                        dma(outT.rearrange('(c p) t -> p c t', p=128)[:, :, sb * 512:(sb + 1) * 512], x3,
                            [xR], list(outR[sb]), 'fo%d' % k)
                    ld(0)
                    ld(1)
                    for s in range(NSF):
                        fin(s)
                        if s + 2 < NSF:
                            ld(s + 2)
                return fn

            NB = T // TB
            slots = {}
            pend = []

            def ld_fn(b, ss):
                def fn(_w):
                    for s_ in ss:
                        sb = b * NSF + s_
                        k = nxt('xn', 2)
                        xs, xR = xn[k]
                        x3 = xs[:].rearrange('p (c t) -> p c t', c=8)
                        dma(x3, src.rearrange('(c p) t -> p c t', p=128)[:, :, sb * 512:(sb + 1) * 512],
                            list(srcR[sb]), [xR], 'xn%d' % k)
                        slots[(b, s_)] = k
                return fn

            def nm_fn(b, s_):
                def fn(_w):
                    xs, xR = xn[slots[(b, s_)]]
                    x3 = xs[:].rearrange('p (c t) -> p c t', c=8)
                    norm_core(x3, xR, 8, gcol, st['hn'][:, :, s_ * 512:(s_ + 1) * 512], st['hnR'][s_],
                              float(D), st['sq'], pend if b > 0 else None)
                return fn

            if NSF != 4:
                step([], norm_fn(0))
            else:
                step([], ld_fn(0, [0, 1]))
                step([], nm_fn(0, 0))
                step([], ld_fn(0, [2]))
                step([], nm_fn(0, 1))
                step([], ld_fn(0, [3]))
                step([], nm_fn(0, 2))
                step([], nm_fn(0, 3))
            for b in range(NB):
                for j in range(NJ):
                    step([('f', f, 'in', j)], up_fn(b, j))
                    if final and b > 0 and j == 5:
                        step([], final_fn(b - 1))
                for i in range(8):
                    nb_ = b + 1 < NB
                    if NSF == 4 and nb_ and i == 0:
                        step([], ld_fn(b + 1, [0, 1]))
                    step([('f', f, 'out', i)], down_fn(b, i))
                    if NSF != 4:
                        if i == 3 and nb_:
                            step([], norm_fn(b + 1))
                    elif nb_:
                        if i == 2:
                            step([], nm_fn(b + 1, 0))
                        elif i == 3:
                            step([], nm_fn(b + 1, 1))
                            step([], ld_fn(b + 1, [2]))
                        elif i == 4:
                            step([], nm_fn(b + 1, 2))
                            step([], ld_fn(b + 1, [3]))
                        elif i == 5:
                            step([], nm_fn(b + 1, 3))
            if final:
                step([], final_fn(NB - 1))

        def wo_load(st, i, src, srcR, t0):
            k = nxt('xn', 2)
            xs, xR = xn[k]
            dma(xs[:, 0:2048], src[i * 128:(i + 1) * 128, t0 * 512:(t0 + 4) * 512],
                [srcR[t0 + sbl][i] for sbl in range(4)], [xR], 'xn%d' % k)
            st['wo_slot'][i] = k

        def wo_fn(st, i, src, srcR, dst, dstR, t0):
            def fn(w):
                wb, wR = w[0]
                w3 = wb[:, 0:1024].rearrange('p (a n) -> p a n', a=8)
                ao = st['ao']
                if i == 0:
                    st['wo_slot'] = {}
                    wo_load(st, 0, src, srcR, t0)
                if i + 1 < 8:
                    wo_load(st, i + 1, src, srcR, t0)
                k = st['wo_slot'][i]
                xs, xR = xn[k]
                for sbl in range(4):
                    kk = nxt('pj', 2)
                    po, poR = P[kk]
                    for a in range(8):
                        MM(po[:], w3[:, a, :], ao[:, a, sbl * 512:(sbl + 1) * 512], a == 0, a == 7,
                           [wR, st['aoR'][sbl]], [poR])
                    r = xs[:, sbl * 512:(sbl + 1) * 512]
                    sch.add('dve', lambda e, r=r, po=po: e.tensor_tensor(r, po[:], r, ALU.add), [poR, xR], [xR])
                dma(dst[i * 128:(i + 1) * 128, t0 * 512:(t0 + 4) * 512], xs[:, 0:2048],
                    [xR], [dstR[t0 + sbl][i] for sbl in range(4)], 'xs%d' % k)
            return fn

        def a_stage(seq, src, srcR, dst, dstR):
            st = {}
            t0 = seq * 4

            def setup(_w):
                new_stage()
                st['sq'] = [(a16(4096).rearrange('p (c t) -> p c t', c=8), AR()) for _ in range(2)]
                st['hn'] = a16(8 * S).rearrange('p (c t) -> p c t', c=8)
                st['hnR'] = [AR() for _ in range(4)]
                st['ao'] = a16(8 * S).rearrange('p (c t) -> p c t', c=8)
                st['aoR'] = [AR() for _ in range(4)]
                st['qkv'] = []
                for _ in range(2):
                    qb_ = (a16(2 * S).rearrange('p (h t) -> p h t', h=2), AR())
                    kb_ = (a16(S), AR())
                    vb_ = (a16(S).rearrange('p (t n) -> p t n', n=128), AR())
                    st['qkv'].append((qb_, kb_, vb_))
                    q2_, q2R_ = qb_
                    sch.add('pool', (lambda q2_: lambda e: e.memset(q2_[64:128, 0, :], 0.0))(q2_), [], [q2R_])
                    sch.add('pool', (lambda q2_: lambda e: e.memset(q2_[0:64, 1, :], 0.0))(q2_), [], [q2R_])
                st['pT'] = [(a16(512), AR()) for _ in range(3)]
                st['bias'] = [(a32(512), AR()) for _ in range(2)]
                st['tmp'] = [(a32(256), AR()) for _ in range(3)]
                st['btb'] = [(a16(512), AR()) for _ in range(2)]
                for sbl in range(4):
                    load_norm(src, srcR, t0 + sbl, G_MIX[0], st['hn'][:, :, sbl * 512:(sbl + 1) * 512],
                              st['hnR'][sbl], st['sq'])
            step([], setup)

            def proj_groups(a, w):
                wqk, wqkR = w[0]
                wv, wvR = w[1]
                wqk3 = wqk[:, 0:2048].rearrange('p (c n) -> p c n', c=8)
                wv3 = wv[:, 0:1024].rearrange('p (c n) -> p c n', c=8)
                hn = st['hn']
                (qT, qR), (kT, kR), (v3, vR) = st['qkv'][a % 2]
                pp, ppR = P[1]
                groups = []
                for sbl in range(4):
                    cs = slice(sbl * 512, (sbl + 1) * 512)
                    hR = st['hnR'][sbl]

                    def gq(cs=cs, hR=hR):
                        for c in range(8):
                            MM(pp[:], wqk3[:, c, 0:128], hn[:, c, cs], c == 0, c == 7, [wqkR, hR], [ppR])
                        sch.add('dve', lambda e: e.tensor_scalar(
                            qT[0:64, 0, cs], pp[0:64, :], 0.125, None, ALU.mult), [ppR], [qR])
                        sch.add('dve', lambda e: e.tensor_scalar(
                            qT[64:128, 1, cs], pp[64:128, :], 0.125, None, ALU.mult), [ppR], [qR])

                    def gk(cs=cs, hR=hR):
                        for c in range(8):
                            MM(pp[:], wqk3[:, c, 128:256], hn[:, c, cs], c == 0, c == 7, [wqkR, hR], [ppR])
                        sch.add('dve', lambda e: e.tensor_copy(kT[:, cs], pp[:]), [ppR], [kR])

                    def gv(sbl=sbl, hR=hR):
                        for tt in range(4):
                            tok = slice(sbl * 512 + tt * 128, sbl * 512 + (tt + 1) * 128)
                            for c in range(8):
                                MM(pp[:, tt * 128:(tt + 1) * 128], hn[:, c, tok], wv3[:, c, :], c == 0, c == 7,
                                   [wvR, hR], [ppR])
                        sch.add('dve', lambda e: e.tensor_copy(
                            v3[:, sbl * 4:(sbl + 1) * 4, :], pp[:].rearrange('p (t n) -> p t n', n=128)),
                            [ppR], [vR])
                    groups += [gq, gk, gv]
                return groups

            def prologue_fn(w):
                for g in proj_groups(0, w):
                    g()
            step([('a', 'qk', 0), ('a', 'v', 0)], prologue_fn)

            def pair_fn(a):
                def fn(w):
                    nxt_groups = proj_groups(a + 1, w) if a + 1 < 8 else []
                    (qT, qR), (kT, kR), (v3, vR) = st['qkv'][a % 2]
                    bk = nxt('bias', 2)
                    bt, btR = st['bias'][bk]
                    dma(bt, bias_d[a], [], [btR], 'bias%d' % bk)
                    btb, btbR = st['btb'][bk]
                    for hh_ in range(2):
                        sch.add('dve', (lambda hh_: lambda e: e.tensor_scalar(
                            btb[:, hh_ * 256:(hh_ + 1) * 256], bt[:, hh_ * 256:(hh_ + 1) * 256],
                            cvec[:, 2 * a + hh_:2 * a + hh_ + 1], None, ALU.subtract))(hh_), [btR, cvR], [btbR])
                    units = []
                    for QB in range(4):
                        for hh in range(2):
                            rs_ = [r for r in (3, 4, 0, 1, 2, 5, 6, 7) if 4 * QB - 4 + r >= 0]
                            for n_, r in enumerate(rs_):
                                units.append((QB, hh, r, n_ == 0, n_ == len(rs_) - 1))
                    SB3 = [P[3], P[4], P[5]]

                    def rng(r):
                        return (0, (r + 1) * 128) if r <= 3 else ((r - 4) * 128, 512)

                    def scores(u):
                        QB, hh, r, first, last = units[u]
                        kt = 4 * QB - 4 + r
                        c0, c1 = rng(r)
                        kb = u % 3
                        sp_, spR = SB3[kb]
                        pT, pTR = st['pT'][kb]
                        tmp, tmpR = st['tmp'][kb]
                        h = 2 * a + hh
                        has_g = r >= 3
                        MM(sp_[:, c0:c1], kT[:, kt * 128:(kt + 1) * 128], qT[:, hh, QB * 512 + c0:QB * 512 + c1],
                           True, not has_g, [kR, qR], [spR])
                        if r >= 4:
                            g0, g1 = c0, min(c0 + 256, 512)
                            b0 = hh * 256
                            k0, k1 = g1, 512
                        elif r == 3:
                            g0, g1 = 0, 128
                            b0 = hh * 256 + 128
                            k0, k1 = 128, c1
                        else:
                            g0 = g1 = 0
                            b0 = 0
                            k0, k1 = 0, c1
                        if g1 > g0:
                            MM(sp_[:, g0:g1], ident[:], btb[:, b0:b0 + (g1 - g0)], False, True,
                               [identR, btbR], [spR])
                        sch.add('act', lambda e: e.activation(
                            pT[:, c0:c1], sp_[:, c0:c1], AF.Exp, bias=cvec[:, h:h + 1]), [spR, cvR], [pTR])
                        if r >= 4:
                            sch.add('pool', lambda e: e.memset(pT[64:128, c0:c0 + 64], 0.0), [], [pTR])
                        if r <= 3:
                            sch.add('pool', lambda e: e.memset(pT[0:64, c1 - 64:c1], 0.0), [], [pTR])

                    def pv_(u):
                        QB, hh, r, first, last = units[u]
                        kt = 4 * QB - 4 + r
                        c0, c1 = rng(r)
                        pl = slice(hh * 64, hh * 64 + 64)
                        kb = u % 3
                        pT, pTR = st['pT'][kb]
                        ob = (2 * QB + hh) % 2
                        po, poR = P[2] if ob == 0 else P[6]
                        pd, pdR = P[7] if ob == 0 else P[0]
                        MM(po[:, c0:c1], v3[:, kt, :], pT[:, c0:c1], first, last, [vR, pTR], [poR])
                        MM(pd[:, c0:c1], ones[:], pT[:, c0:c1], first, last, [onesR, pTR], [pdR])
                        if last:
                            rk = nxt('rstd', 2)
                            rd_, rdR = rstd[rk]
                            RCP(rd_[pl, :], pd[pl, :], [pdR], [rdR], 'dve' if hh == 0 else 'act')
                            ao = st['ao']
                            sch.add('dve', lambda e: e.tensor_tensor(
                                ao[pl, a, QB * 512:(QB + 1) * 512], po[pl, :], rd_[pl, :], ALU.mult),
                                [poR, rdR], [st['aoR'][QB]])

                    scores(0)
                    scores(1)
                    gi = 0
                    for u in range(len(units)):
                        if u + 2 < len(units):
                            scores(u + 2)
                        pv_(u)
                        want = (len(nxt_groups) * (u + 1)) // len(units)
                        while gi < want:
                            nxt_groups[gi]()
                            gi += 1
                return fn

            for a in range(8):
                step([('a', 'qk', a + 1), ('a', 'v', a + 1)] if a + 1 < 8 else [], pair_fn(a))
            for i in range(8):
                step([('a', 'o', i)], wo_fn(st, i, src, srcR, dst, dstR, t0))

        def b_stage(seq, kvsrc, kvsrcR, src, srcR, dst, dstR):
            st = {}
            t0 = seq * 4
            scale = float(192 ** -0.5)

            def load_tab(sbl):
                tk = nxt('tab', 2)
                tb, tbR = st['tab'][tk]
                dma(tb[0:64, 0:512], cos_d[:, sbl * 512:(sbl + 1) * 512], [], [tbR], 'tab%d' % tk)
                dma(tb[0:64, 512:1024], sin_d[:, sbl * 512:(sbl + 1) * 512], [], [tbR], 'tab%d' % tk)
                return tb, tbR

            def rope_out(pr, prR, psw, pswR, tb, tbR, out_ap, outR):
                if nxt('ropet', 2) == 0:
                    (t1, t1R), (t2, t2R) = sg[0], sg[1]
                else:
                    (t1, t1R), (t2, t2R) = rstd[0], rstd[1]
                sch.add('dve', lambda e: e.tensor_tensor(t1[0:64, :], pr[0:64, :], tb[0:64, 0:512], ALU.mult),
                        [prR, tbR], [t1R])
                sch.add('dve', lambda e: e.tensor_tensor(t2[0:64, :], psw[0:64, :], tb[0:64, 512:1024], ALU.mult),
                        [pswR, tbR], [t2R])
                sch.add('dve', lambda e: e.tensor_tensor(out_ap, t1[0:64, :], t2[0:64, :], ALU.add),
                        [t1R, t2R], [outR])

            def setup(_w):
                new_stage()
                st['sq'] = [(a16(4096).rearrange('p (c t) -> p c t', c=8), AR()) for _ in range(2)]
                st['ao'] = a16(8 * S).rearrange('p (c t) -> p c t', c=8)
                st['aoR'] = [AR() for _ in range(4)]
                st['cq'] = a16(6 * S).rearrange('p (c t) -> p c t', c=6)
                st['cqR'] = [AR() for _ in range(4)]
                st['ckv'] = a16(2 * S).rearrange('p (c t) -> p c t', c=2)
                st['ckvR'] = [AR() for _ in range(4)]
                st['kr'] = (a16(S), AR())
                st['kn'] = (a16(S), AR())
                st['vh'] = (a16(S).rearrange('p (t n) -> p t n', n=128), AR())
                st['qn'] = (a16(S), AR())
                st['qr'] = (a16(S), AR())
                st['pT'] = [(a16(512), AR()) for _ in range(3)]
                for nm in ('kr', 'qr'):
                    buf, bR = st[nm]
                    sch.add('pool', (lambda buf: lambda e: e.memset(buf[64:128, :], 0.0))(buf), [], [bR])
                st['tab'] = [(a32(1024), AR()) for _ in range(2)]
                for sbl in range(4):
                    load_norm(kvsrc, kvsrcR, t0 + sbl, G_KV, st['ao'][:, :, sbl * 512:(sbl + 1) * 512],
                              st['aoR'][sbl], st['sq'])
            step([], setup)

            def ckv_fn(w):
                wb, wR = w[0]
                w3 = wb[:, 0:2048].rearrange('p (c n) -> p c n', c=8)
                hn = st['ao']
                for sbl in range(4):
                    cs = slice(sbl * 512, (sbl + 1) * 512)
                    xk = nxt('xn', 2)
                    xs, xR = xn[xk]
                    x3 = xs[:].rearrange('p (c t) -> p c t', c=8)
                    for m in range(2):
                        k = nxt('pj', 2)
                        pp, ppR = P[k]
                        for c in range(8):
                            MM(pp[:], w3[:, c, m * 128:(m + 1) * 128], hn[:, c, cs], c == 0, c == 7,
                               [wR, st['aoR'][sbl]], [ppR])
                        sch.add('dve', lambda e, pp=pp, m=m, x3=x3: e.tensor_copy(x3[:, m, :], pp[:]), [ppR], [xR])
                    norm_core(x3[:, 0:2, :], xR, 2, G_LAT, st['ckv'][:, :, cs], st['ckvR'][sbl], 256.0, st['sq'])
            step([('b', 'dckv')], ckv_fn)

            def krope_fn(w):
                wb, wR = w[0]
                w3 = wb[:, 0:1024].rearrange('p (c n) -> p c n', c=8)
                hn = st['ao']
                kr, krR = st['kr']
                for sbl in range(4):
                    cs = slice(sbl * 512, (sbl + 1) * 512)
                    tb, tbR = load_tab(sbl)
                    pr, prR = P[2]
                    psw, pswR = P[3]
                    for c in range(8):
                        MM(pr[0:64, :], w3[:, c, 0:64], hn[:, c, cs], c == 0, c == 7, [wR, st['aoR'][sbl]], [prR])
                    for c in range(8):
                        MM(psw[0:64, :], w3[:, c, 64:128], hn[:, c, cs], c == 0, c == 7, [wR, st['aoR'][sbl]], [pswR])
                    rope_out(pr, prR, psw, pswR, tb, tbR, kr[0:64, cs], krR)
            step([('b', 'drope')], krope_fn)

            def qnorm_fn(_w):
                for sbl in range(4):
                    load_norm(src, srcR, t0 + sbl, G_MIX[1], st['ao'][:, :, sbl * 512:(sbl + 1) * 512],
                              st['aoR'][sbl], st['sq'])
            step([], qnorm_fn)

            def dq_fn(w):
                hn = st['ao']
                for sbl in range(4):
                    cs = slice(sbl * 512, (sbl + 1) * 512)
                    xk = nxt('xn', 2)
                    xs, xR = xn[xk]
                    x3 = xs[:].rearrange('p (c t) -> p c t', c=8)
                    for m in range(6):
                        wb, wR = w[m // 2]
                        w3 = wb[:, 0:2048].rearrange('p (c n) -> p c n', c=8)
                        k = nxt('pj', 2)
                        pp, ppR = P[k]
                        for c in range(8):
                            MM(pp[:], w3[:, c, (m % 2) * 128:(m % 2 + 1) * 128], hn[:, c, cs], c == 0, c == 7,
                               [wR, st['aoR'][sbl]], [ppR])
                        sch.add('dve', lambda e, pp=pp, m=m, x3=x3: e.tensor_copy(x3[:, m, :], pp[:]), [ppR], [xR])
                    norm_core(x3[:, 0:6, :], xR, 6, G_QN, st['cq'][:, :, cs], st['cqR'][sbl], 768.0, st['sq'])
            step([('b', 'dq', 0), ('b', 'dq', 1), ('b', 'dq', 2)], dq_fn)

            def head_fn(h):
                def fn(w):
                    wb, wR = w[0]
                    wk = wb[:, 0:256].rearrange('p (c n) -> p c n', c=2)
                    wv = wb[:, 256:512].rearrange('p (c n) -> p c n', c=2)
                    wqn = wb[:, 512:1280].rearrange('p (c n) -> p c n', c=6)
                    wqr = wb[:, 1280:1664].rearrange('p (c n) -> p c n', c=6)
                    wqs = wb[:, 1664:2048].rearrange('p (c n) -> p c n', c=6)
                    ckv = st['ckv']
                    cq = st['cq']
                    kn, knR = st['kn']
                    vh, vhR = st['vh']
                    qn, qnR = st['qn']
                    qr, qrR = st['qr']
                    kr, krR = st['kr']
                    for sbl in range(4):
                        cs = slice(sbl * 512, (sbl + 1) * 512)
                        cR = st['ckvR'][sbl]
                        qR_ = st['cqR'][sbl]
                        k = nxt('pj', 2)
                        pp, ppR = P[k]
                        for c in range(2):
                            MM(pp[:], wk[:, c, :], ckv[:, c, cs], c == 0, c == 1, [wR, cR], [ppR])
                        sch.add('act', lambda e, pp=pp, cs=cs: e.activation(kn[:, cs], pp[:], AF.Copy), [ppR], [knR])
                        k = nxt('pj', 2)
                        pp, ppR = P[k]
                        for tt in range(4):
                            tok = slice(sbl * 512 + tt * 128, sbl * 512 + (tt + 1) * 128)
                            for c in range(2):
                                MM(pp[:, tt * 128:(tt + 1) * 128], ckv[:, c, tok], wv[:, c, :], c == 0, c == 1,
                                   [wR, cR], [ppR])
                        sch.add('act', lambda e, pp=pp, sbl=sbl: e.activation(
                            vh[:, sbl * 4:(sbl + 1) * 4, :], pp[:].rearrange('p (t n) -> p t n', n=128), AF.Copy),
                            [ppR], [vhR])
                        k = nxt('pj', 2)
                        pp, ppR = P[k]
                        for c in range(6):
                            MM(pp[:], wqn[:, c, :], cq[:, c, cs], c == 0, c == 5, [wR, qR_], [ppR])
                        sch.add('act', lambda e, pp=pp, cs=cs: e.activation(qn[:, cs], pp[:], AF.Copy), [ppR], [qnR])
                        tb, tbR = load_tab(sbl)
                        pr, prR = P[2] if sbl % 2 == 0 else P[6]
                        psw, pswR = P[3] if sbl % 2 == 0 else P[7]
                        for c in range(6):
                            MM(pr[0:64, :], wqr[:, c, :], cq[:, c, cs], c == 0, c == 5, [wR, qR_], [prR])
                        for c in range(6):
                            MM(psw[0:64, :], wqs[:, c, :], cq[:, c, cs], c == 0, c == 5, [wR, qR_], [pswR])
                        rope_out(pr, prR, psw, pswR, tb, tbR, qr[0:64, cs], qrR)
                    units = [(Q, kt) for Q in range(4) for kt in range(4 * Q + 4)]

                    def scores(u):
                        Q, kt = units[u]
                        c0 = max(kt - 4 * Q, 0) * 128
                        kb = u % 3
                        sp_, spR = (P[4], P[5], P[0])[kb]
                        qs = slice(Q * 512 + c0, (Q + 1) * 512)
                        MM(sp_[:, c0:512], kn[:, kt * 128:(kt + 1) * 128], qn[:, qs], True, False,
                           [knR, qnR], [spR])
                        MM(sp_[:, c0:512], kr[:, kt * 128:(kt + 1) * 128], qr[:, qs], False, True,
                           [krR, qrR], [spR])
                        pT, pTR = st['pT'][kb]
                        sch.add('act', lambda e: e.activation(pT[:, c0:512], sp_[:, c0:512], AF.Exp, scale=scale),
                                [spR], [pTR])
                        if kt >= 4 * Q:
                            sch.add('pool', lambda e: e.memset(pT[64:128, c0:c0 + 64], 0.0), [], [pTR])

                    def pv_(u):
                        Q, kt = units[u]
                        c0 = max(kt - 4 * Q, 0) * 128
                        kb = u % 3
                        pT, pTR = st['pT'][kb]
                        nkt = 4 * Q + 4
                        po, poR = P[6] if Q % 2 == 0 else P[2]
                        pd, pdR = P[7] if Q % 2 == 0 else P[3]
                        MM(po[:, c0:512], vh[:, kt, :], pT[:, c0:512], kt == 0, kt == nkt - 1, [vhR, pTR], [poR])
                        MM(pd[:, c0:512], ones[:], pT[:, c0:512], kt == 0, kt == nkt - 1, [onesR, pTR], [pdR])
                        if kt == nkt - 1:
                            rk = nxt('rstd', 2)
                            rd_, rdR = rstd[rk]
                            RCP(rd_[:], pd[:], [pdR], [rdR])
                            ao = st['ao']
                            sch.add('dve', lambda e: e.tensor_tensor(
                                ao[:, h, Q * 512:(Q + 1) * 512], po[:], rd_[:], ALU.mult),
                                [poR, rdR], [st['aoR'][Q]])

                    scores(0)
                    scores(1)
                    for u in range(len(units)):
                        if u + 2 < len(units):
                            scores(u + 2)
                        pv_(u)
                return fn

            for h in range(8):
                step([('b', 'head', h)], head_fn(h))
            for i in range(8):
                step([('b', 'o', i)], wo_fn(st, i, src, srcR, dst, dstR, t0))

        xR_in = dram_rs()
        xsR = [dram_rs() for _ in range(6)]
        outR = dram_rs()
        stages = []
        stages.append(lambda dst, dR: ffn_stage(0, xT, xR_in, dst, dR, G_FFN1[0], False))
        stages.append(lambda dst, dR: [a_stage(q, xs_d[0], xsR[0], dst, dR) for q in range(NSEQ)])
        stages.append(lambda dst, dR: ffn_stage(1, xs_d[1], xsR[1], dst, dR, G_FFN2[0], False))
        stages.append(lambda dst, dR: ffn_stage(2, xs_d[2], xsR[2], dst, dR, G_FFN1[1], False))
        stages.append(lambda dst, dR: [b_stage(q, xs_d[2], xsR[2], xs_d[3], xsR[3], dst, dR) for q in range(NSEQ)])
        stages.append(lambda dst, dR: ffn_stage(3, xs_d[4], xsR[4], xs_d[5], xsR[5], G_FFN2[1], True))
        nst = min(upto, len(stages))
        for k in range(nst):
            if k == nst - 1:
                stages[k](outT, outR)
            else:
                stages[k](xs_d[k], xsR[k])

        tile_seq = []
        first_tile = []
        for tiles, fn in steps:
            first_tile.append(len(tile_seq))
            tile_seq.extend(tiles)
        issued = {}
        nissued = 0
        for m, (tiles, fn) in enumerate(steps):
            a = first_tile[m]
            lim = min(len(tile_seq), a + 4)
            while nissued < lim:
                issued[nissued] = issue_tile(tile_seq[nissued], nissued)
                nissued += 1
            fn([issued[a + k] for k in range(len(tiles))])

        final_chans = [ch for ch in sch.chans if ch.startswith('rs') or ch.startswith('fo') or ch.startswith('xs')]
        sch.emit(nc, stack, final_chans)
        stats = {e: len(sch.ops[e]) for e in ENGS}
        stats['maxcnt'] = {e: max([op.cnt for op in sch.ops[e]] + [0]) for e in ENGS}
        stats['maxchan'] = max(v[0] for v in sch.chans.values())
        stats['nsem'] = len(sch.chans) + 5
    return nc, stats


def make_in_maps(inputs):
    inp = {k: np.asarray(v, dtype=np.float32) for k, v in inputs.items()}
    blob, index, widths = build_weight_tiles(inp)
    i2, w2, nt = tile_index()
    assert i2 == index and w2 == widths and nt == blob.shape[0]
    gains, cvec, bias, cosT, sinT = build_consts(inp)
    x = inp['x']
    in_maps = []
    for c in range(NCORES):
        xc = np.ascontiguousarray(x[c * NSEQ:(c + 1) * NSEQ].reshape(T, D).T)
        in_maps.append({"xT": xc, "wblob": blob, "gains": gains, "cvec": cvec, "biasA": bias,
                        "cosT": cosT, "sinT": sinT, "ident": np.eye(128, dtype=np.float32)})
    return in_maps


def kernel(**inputs):
    in_maps = make_in_maps(inputs)
    nc, _ = build_program()
    res = run_bass_kernel_spmd(nc, in_maps, core_ids=list(range(NCORES)))
    outs = []
    for c in range(NCORES):
        o = np.asarray(res.results[c]["outT"], dtype=np.float32)
        outs.append(o.T.reshape(NSEQ, S, D))
    return np.concatenate(outs, axis=0).astype(np.float32)
```
